# Optimizing a Trainium2 kernel written in Bass

```python
import math
import jax, jax.numpy as jnp
from jax import lax
import numpy as np

D_MODEL = 4096
BATCH = 1
SEQ = 8192
DEPTH = 2

CHUNK = 64
N_META = 16
Q_BLOCK = 128
SCAN_CHUNK = 64
N_EVEN = (DEPTH + 1) // 2
N_ODD = DEPTH // 2
NORM_EPS = 1e-6

A_HEADS = D_MODEL // 256
A_DK = 128
A_DV = 128
A_KWIDTH = A_HEADS * A_DK
A_WIDTH = A_HEADS * A_DV

B_HEADS = D_MODEL // 256
B_NOPE = 128
B_ROPE = 64
B_DV = 128
B_Q_LORA = 1024
B_KV_LORA = 512
B_WIDTH = B_HEADS * B_DV
ROPE_THETA = 10000.0

C_GROUPS = 4
POOL_WINDOWS = (2, 4, 8, 16)
C_WIDTH = D_MODEL // 4
C_GROUP_DIM = C_WIDTH // C_GROUPS

D_INNER = 3 * D_MODEL // 4
D_HEAD_DIM = 64
D_HEADS = D_INNER // D_HEAD_DIM
D_GROUPS = 8
D_STATE = 128
D_CONV = 4
D_XBC = D_INNER + 2 * D_GROUPS * D_STATE

D_FF = 256 * ((8 * D_MODEL // 3 + 255) // 256)
FFN_CONV = 3

AB_IN = 2 * A_KWIDTH + 2 * A_WIDTH + B_Q_LORA + B_KV_LORA + B_ROPE
CD_IN = C_WIDTH + D_INNER + D_XBC + D_HEADS

kernel_name = 'hybrid_hgrn2_mla_pool_ssd_convffn'


def rmsnorm(x, g):
    xf = x.astype(jnp.float32)
    y = xf * lax.rsqrt(jnp.mean(xf * xf, axis=-1, keepdims=True) + NORM_EPS)
    return (y * g.astype(jnp.float32)).astype(x.dtype)


def split_cols(a, sizes):
    idx = [int(v) for v in np.cumsum(sizes)[:-1]]
    return jnp.split(a, idx, axis=-1)


def causal_dwconv(x, w, b):
    k_width, ch = w.shape
    y = lax.conv_general_dilated(x, w[:, None, :].astype(x.dtype), window_strides=(1,),
                                 padding=((k_width - 1, 0),),
                                 dimension_numbers=('NWC', 'WIO', 'NWC'),
                                 feature_group_count=ch)
    return y + b.astype(x.dtype)


def rope_tables(length):
    inv = ROPE_THETA ** (-jnp.arange(0, B_ROPE, 2, dtype=jnp.float32) / B_ROPE)
    ang = jnp.arange(length, dtype=jnp.float32)[:, None] * inv[None, :]
    return jnp.cos(ang), jnp.sin(ang)


def apply_rope(x, cos, sin):
    xf = x.astype(jnp.float32)
    half = B_ROPE // 2
    x1, x2 = xf[..., :half], xf[..., half:]
    c = cos[None, :, None, :]
    s = sin[None, :, None, :]
    return jnp.concatenate([x1 * c - x2 * s, x1 * s + x2 * c], axis=-1).astype(x.dtype)


def hgrn2_chunkwise(q, log_f, k, v):
    bsz, length, heads, dk = q.shape
    dv = v.shape[-1]
    n = length // SCAN_CHUNK

    def to_chunks(t):
        return t.reshape(bsz, n, SCAN_CHUNK, heads, t.shape[-1]).transpose(1, 0, 3, 2, 4)

    tril = jnp.tril(jnp.ones((SCAN_CHUNK, SCAN_CHUNK), dtype=bool))

    def step(state, inp):
        qc, lfc, kc, vc = inp
        b = jnp.cumsum(lfc, axis=2)
        o_inter = jnp.einsum('bhtk,bhkv->bhtv', qc * jnp.exp(b), state)
        rel = b[:, :, :, None, :] - b[:, :, None, :, :]
        dec = jnp.exp(jnp.where(tril[:, :, None], rel, -jnp.inf))
        attn = jnp.einsum('bhik,bhjk,bhijk->bhij', qc, kc, dec)
        o_intra = jnp.einsum('bhij,bhjv->bhiv', attn, vc)
        b_last = b[:, :, -1:, :]
        new_state = (jnp.exp(b_last[:, :, 0, :])[..., None] * state
                     + jnp.einsum('bhtk,bhtv->bhkv', kc * jnp.exp(b_last - b), vc))
        return new_state, o_inter + o_intra

    s0 = jnp.zeros((bsz, heads, dk, dv), jnp.float32)
    _, o = lax.scan(step, s0, (to_chunks(q), to_chunks(log_f), to_chunks(k), to_chunks(v)))
    return o.transpose(1, 0, 3, 2, 4).reshape(bsz, length, heads, dv)


def block_causal_attention(q, k, v, chunk_id):
    bsz, length, heads, dqk = q.shape
    nb = length // Q_BLOCK
    scale = dqk ** -0.5
    qb = q.reshape(bsz, nb, Q_BLOCK, heads, dqk).transpose(1, 0, 2, 3, 4)
    cb = chunk_id.reshape(nb, Q_BLOCK)

    def one_block(args):
        qi, ci = args
        s = jnp.einsum('bqhd,bkhd->bhqk', qi, k, preferred_element_type=jnp.float32) * scale
        mask = chunk_id[None, :] <= ci[:, None]
        p = jax.nn.softmax(jnp.where(mask[None, None], s, -jnp.inf), axis=-1)
        return jnp.einsum('bhqk,bkhd->bqhd', p.astype(v.dtype), v)

    o = lax.map(one_block, (qb, cb))
    return o.transpose(1, 0, 2, 3, 4).reshape(bsz, length, heads * v.shape[-1])


def multiscale_pool(u, pool_w, pool_scale):
    bsz, length, _ = u.shape
    groups = u.astype(jnp.float32).reshape(bsz, length, C_GROUPS, C_GROUP_DIM)
    cs = jnp.cumsum(groups, axis=1)
    t = jnp.arange(length)
    outs = []
    for gi, w in enumerate(POOL_WINDOWS):
        c = cs[:, :, gi]
        lagged = jnp.pad(c, ((0, 0), (w, 0), (0, 0)))[:, :length]
        cnt = jnp.minimum(t + 1, w).astype(jnp.float32)[None, :, None]
        outs.append((c - lagged) / cnt - groups[:, :, gi])
    pooled = jnp.stack(outs, axis=2)
    y = jnp.einsum('blgc,gcd->blgd', pooled, pool_w.astype(jnp.float32)).reshape(bsz, length, C_WIDTH)
    return (y * pool_scale.astype(jnp.float32)).astype(u.dtype)


def ssd_chunked(xh, dt, a, bm, cm):
    bsz, length, heads, hp = xh.shape
    hg = heads // D_GROUPS
    n = length // SCAN_CHUNK
    T = SCAN_CHUNK
    xd = (xh * dt[..., None]).reshape(bsz, n, T, D_GROUPS, hg, hp)
    la = (dt * a).reshape(bsz, n, T, D_GROUPS, hg).transpose(0, 3, 4, 1, 2)
    a_cs = jnp.cumsum(la, axis=-1)
    bc = bm.reshape(bsz, n, T, D_GROUPS, D_STATE)
    cc = cm.reshape(bsz, n, T, D_GROUPS, D_STATE)
    tril = jnp.tril(jnp.ones((T, T), dtype=bool))
    seg = a_cs[..., :, None] - a_cs[..., None, :]
    lmat = jnp.exp(jnp.where(tril, seg, -jnp.inf))
    cb = jnp.einsum('bclgn,bcsgn->bcgls', cc, bc)
    y_diag = jnp.einsum('bcgls,bghcls,bcsghp->bclghp', cb, lmat, xd)
    decay_states = jnp.exp(a_cs[..., -1:] - a_cs)
    states = jnp.einsum('bcsgn,bghcs,bcsghp->bcghpn', bc, decay_states, xd)
    chunk_decay = jnp.exp(a_cs[..., -1])

    def step(hstate, inp):
        st, dec = inp
        return dec[..., None, None] * hstate + st, hstate

    h0 = jnp.zeros((bsz, D_GROUPS, hg, hp, D_STATE), jnp.float32)
    _, prev = lax.scan(step, h0, (states.transpose(1, 0, 2, 3, 4, 5), chunk_decay.transpose(3, 0, 1, 2)))
    prev = prev.transpose(1, 0, 2, 3, 4, 5)
    y_off = jnp.einsum('bclgn,bcghpn,bghcl->bclghp', cc, prev, jnp.exp(a_cs))
    return (y_diag + y_off).reshape(bsz, length, heads, hp)


def mixer_hgrn2_mla(xn, lb, w_in, hgrn_g, q_norm_g, kv_norm_g, w_uq, w_ukv, w_out, cos, sin, chunk_id):
    bsz, length, _ = xn.shape
    proj = xn @ w_in
    q_a, f_a, i_a, g_a, c_q, c_kv, k_r = split_cols(
        proj, [A_KWIDTH, A_KWIDTH, A_WIDTH, A_WIDTH, B_Q_LORA, B_KV_LORA, B_ROPE])
    lbh = lb.reshape(A_HEADS, A_DK)
    fp = f_a.astype(jnp.float32).reshape(bsz, length, A_HEADS, A_DK)
    log_f = jnp.log(lbh + (1.0 - lbh) * jax.nn.sigmoid(fp))
    k_a = (1.0 - lbh) * jax.nn.sigmoid(-fp)
    o_a = hgrn2_chunkwise(q_a.astype(jnp.float32).reshape(bsz, length, A_HEADS, A_DK), log_f, k_a,
                          i_a.astype(jnp.float32).reshape(bsz, length, A_HEADS, A_DV))
    o_a = rmsnorm(o_a, hgrn_g).reshape(bsz, length, A_WIDTH)
    o_a = (o_a * jax.nn.silu(g_a.astype(jnp.float32))).astype(xn.dtype)
    qf = (rmsnorm(c_q, q_norm_g) @ w_uq).reshape(bsz, length, B_HEADS, B_NOPE + B_ROPE)
    kvf = (rmsnorm(c_kv, kv_norm_g) @ w_ukv).reshape(bsz, length, B_HEADS, B_NOPE + B_DV)
    q_pe = apply_rope(qf[..., B_NOPE:], cos, sin)
    k_pe = apply_rope(k_r[:, :, None, :], cos, sin)
    q = jnp.concatenate([qf[..., :B_NOPE], q_pe], axis=-1)
    k = jnp.concatenate([kvf[..., :B_NOPE], jnp.broadcast_to(k_pe, (bsz, length, B_HEADS, B_ROPE))], axis=-1)
    o_b = block_causal_attention(q, k, kvf[..., B_NOPE:], chunk_id)
    return jnp.concatenate([o_a, o_b.astype(xn.dtype)], axis=-1) @ w_out


def mixer_pool_ssd(xn, w_in, pool_w, pool_scale, conv_w, conv_b, dt_bias, a_log, d_skip, norm_g, w_out):
    bsz, length, _ = xn.shape
    proj = xn @ w_in
    u_c, z, xbc, dt_raw = split_cols(proj, [C_WIDTH, D_INNER, D_XBC, D_HEADS])
    o_c = multiscale_pool(u_c, pool_w, pool_scale)
    xbc = jax.nn.silu(causal_dwconv(xbc, conv_w, conv_b))
    xs, bm, cm = split_cols(xbc, [D_INNER, D_GROUPS * D_STATE, D_GROUPS * D_STATE])
    dt = jax.nn.softplus(dt_raw.astype(jnp.float32) + dt_bias.astype(jnp.float32))
    a = -jnp.exp(a_log.astype(jnp.float32))
    xh = xs.astype(jnp.float32).reshape(bsz, length, D_HEADS, D_HEAD_DIM)
    y = ssd_chunked(xh, dt, a,
                    bm.astype(jnp.float32).reshape(bsz, length, D_GROUPS, D_STATE),
                    cm.astype(jnp.float32).reshape(bsz, length, D_GROUPS, D_STATE))
    y = (y + xh * d_skip.astype(jnp.float32)[:, None]).reshape(bsz, length, D_INNER)
    y = rmsnorm(y * jax.nn.silu(z.astype(jnp.float32)), norm_g).astype(xn.dtype)
    return jnp.concatenate([o_c, y], axis=-1) @ w_out


def conv_ffn(xn, w_up, conv_w, conv_b, w_down):
    u = causal_dwconv(xn @ w_up, conv_w, conv_b)
    gate, val = jnp.split(u, 2, axis=-1)
    return (jax.nn.silu(gate) * val) @ w_down


def setup_inputs(seed: int = 0) -> dict:
    key = jax.random.key(seed)
    ks = jax.random.split(key, 25)
    f32 = jnp.float32

    def nrm(k, shape, scale):
        return jax.random.normal(k, shape, f32) * scale

    def gain(k, shape):
        return 1.0 + 0.05 * jax.random.normal(k, shape, f32)

    dt0 = jnp.exp(jax.random.uniform(ks[16], (N_ODD, D_HEADS), f32)
                  * (math.log(0.1) - math.log(0.001)) + math.log(0.001))
    return {
        'x': nrm(ks[0], (BATCH, SEQ, D_MODEL), 1.0),
        'meta_tokens': nrm(ks[1], (N_META, D_MODEL), 1.0),
        'lb_logits': nrm(ks[2], (DEPTH + 1, A_KWIDTH), 0.1),
        'norm_g': gain(ks[3], (DEPTH, 4, D_MODEL)),
        'ab_w_in': nrm(ks[4], (N_EVEN, D_MODEL, AB_IN), D_MODEL ** -0.5),
        'hgrn_norm_g': gain(ks[5], (N_EVEN, A_HEADS, A_DV)),
        'mla_q_norm_g': gain(ks[6], (N_EVEN, B_Q_LORA)),
        'mla_kv_norm_g': gain(ks[7], (N_EVEN, B_KV_LORA)),
        'mla_w_uq': nrm(ks[8], (N_EVEN, B_Q_LORA, B_HEADS * (B_NOPE + B_ROPE)), B_Q_LORA ** -0.5),
        'mla_w_ukv': nrm(ks[9], (N_EVEN, B_KV_LORA, B_HEADS * (B_NOPE + B_DV)), B_KV_LORA ** -0.5),
        'ab_w_out': nrm(ks[10], (N_EVEN, A_WIDTH + B_WIDTH, D_MODEL), (A_WIDTH + B_WIDTH) ** -0.5),
        'cd_w_in': nrm(ks[11], (N_ODD, D_MODEL, CD_IN), D_MODEL ** -0.5),
        'pool_w': nrm(ks[12], (N_ODD, C_GROUPS, C_GROUP_DIM, C_GROUP_DIM), C_GROUP_DIM ** -0.5),
        'pool_scale': gain(ks[13], (N_ODD, C_WIDTH)),
        'ssm_conv_w': nrm(ks[14], (N_ODD, D_CONV, D_XBC), D_CONV ** -0.5),
        'ssm_conv_b': nrm(ks[15], (N_ODD, D_XBC), 0.02),
        'ssm_dt_bias': dt0 + jnp.log(-jnp.expm1(-dt0)),
        'ssm_a_log': jnp.log(jax.random.uniform(ks[17], (N_ODD, D_HEADS), f32, 1.0, 16.0)),
        'ssm_d': gain(ks[18], (N_ODD, D_HEADS)),
        'ssm_norm_g': gain(ks[19], (N_ODD, D_INNER)),
        'cd_w_out': nrm(ks[20], (N_ODD, C_WIDTH + D_INNER, D_MODEL), (C_WIDTH + D_INNER) ** -0.5),
        'ffn_w_up': nrm(ks[21], (DEPTH, D_MODEL, 2 * D_FF), D_MODEL ** -0.5),
        'ffn_conv_w': nrm(ks[22], (DEPTH, FFN_CONV, 2 * D_FF), FFN_CONV ** -0.5),
        'ffn_conv_b': nrm(ks[23], (DEPTH, 2 * D_FF), 0.02),
        'ffn_w_down': nrm(ks[24], (DEPTH, D_FF, D_MODEL), D_FF ** -0.5),
    }


def reference(x, meta_tokens, lb_logits, norm_g, ab_w_in, hgrn_norm_g, mla_q_norm_g, mla_kv_norm_g,
              mla_w_uq, mla_w_ukv, ab_w_out, cd_w_in, pool_w, pool_scale, ssm_conv_w, ssm_conv_b,
              ssm_dt_bias, ssm_a_log, ssm_d, ssm_norm_g, cd_w_out, ffn_w_up, ffn_conv_w, ffn_conv_b,
              ffn_w_down):
    bsz, seq, _ = x.shape
    total = N_META + seq
    lp = -(-total // Q_BLOCK) * Q_BLOCK
    meta = jnp.broadcast_to(meta_tokens.astype(x.dtype)[None], (bsz, N_META, D_MODEL))
    h = jnp.concatenate([meta, x], axis=1)
    h = jnp.pad(h, ((0, 0), (0, lp - total), (0, 0)))
    pos = jnp.arange(lp, dtype=jnp.int32)
    chunk_id = jnp.where(pos < N_META, 0, 1 + (pos - N_META) // CHUNK)
    cos, sin = rope_tables(lp)
    lb_all = jnp.cumsum(jax.nn.softmax(lb_logits.astype(jnp.float32), axis=0), axis=0)
    for layer in range(DEPTH):
        g = norm_g[layer]
        j = layer // 2
        xn = rmsnorm(h, g[0])
        if layer % 2 == 0:
            mixed = mixer_hgrn2_mla(xn, lb_all[layer], ab_w_in[j], hgrn_norm_g[j], mla_q_norm_g[j],
                                    mla_kv_norm_g[j], mla_w_uq[j], mla_w_ukv[j], ab_w_out[j],
                                    cos, sin, chunk_id)
        else:
            mixed = mixer_pool_ssd(xn, cd_w_in[j], pool_w[j], pool_scale[j], ssm_conv_w[j], ssm_conv_b[j],
                                   ssm_dt_bias[j], ssm_a_log[j], ssm_d[j], ssm_norm_g[j], cd_w_out[j])
        h = h + rmsnorm(mixed, g[1])
        ff = conv_ffn(rmsnorm(h, g[2]), ffn_w_up[layer], ffn_conv_w[layer], ffn_conv_b[layer], ffn_w_down[layer])
        h = h + rmsnorm(ff, g[3])
    return h[:, N_META:total]
```

```python
import contextlib
import numpy as np
import ml_dtypes
import concourse.bass as bass
import concourse.mybir as mybir
from concourse.bass_utils import run_bass_kernel_spmd

F32 = mybir.dt.float32
BF16 = mybir.dt.bfloat16
AF = mybir.ActivationFunctionType
ALU = mybir.AluOpType
NPBF = ml_dtypes.bfloat16

D = 4096
TOK = 1040
CH = [(0, 347), (347, 347), (694, 346)]
NCORE = 8
SEQ = 8192
NMETA = 16
LTOT = 8208
PADL = 112
LP = 8320
DFF = 11008
EPS = 1e-6

COMPUTE = ("pe", "act", "dve", "pool")


class Buf:
    __slots__ = ("name", "w", "r", "dcount")

    def __init__(self, name):
        self.name = name
        self.w = None
        self.r = {}
        self.dcount = 0


class T:
    def __init__(self, ten, name):
        self.t = ten
        self.b = Buf(name)
        self.subs = {}

    def __getitem__(self, k):
        return self.t[k]

    def sub(self, key):
        s = self.subs.get(key)
        if s is None:
            s = T(self.t, "%s.%s" % (self.b.name, key))
            self.subs[key] = s
        return s


class _Recorder:
    def __getattr__(self, name):
        def cap(*a, **k):
            self.call = (name, a, k)
        return cap

    def replay(self, e):
        name, a, k = self.call
        return getattr(e, name)(*a, **k)


class Prog:
    def __init__(self, nc, stack):
        self.nc = nc
        self.stack = stack
        self.q = {e: [] for e in ("pe", "act", "dve", "pool", "sp")}
        self.waited = {e: {} for e in self.q}
        self.dma_sems = {}
        self.dma_final = {}

    def sb(self, name, shape, dt=F32):
        return T(self.stack.enter_context(self.nc.sbuf_tensor(name, list(shape), dt)), name)

    def ps(self, name, shape, dt=F32):
        return T(self.stack.enter_context(self.nc.psum_tensor(name, list(shape), dt)), name)

    def dram(self, name, shape, dt=F32, kind="Internal"):
        return T(self.nc.dram_tensor(name, list(shape), dt, kind=kind).ap(), name)

    def _collect(self, eng, reads, writes):
        need = {}

        def add(ev, is_writer):
            if ev is None:
                return
            kind, k, v = ev
            if kind == "c" and k == eng:
                if eng == "pe" or not is_writer:
                    return
            key = (kind, k)
            if need.get(key, -1) < v:
                need[key] = v

        for b in reads:
            add(b.b.w, True)
        for b in writes:
            add(b.b.w, True)
            for ev in b.b.r.values():
                add(ev, False)
        waits = []
        wd = self.waited[eng]
        for key, v in need.items():
            if wd.get(key, -1) >= v:
                continue
            wd[key] = v
            waits.append((key, v))
        return waits

    def _commit(self, ev, reads, writes):
        for b in writes:
            b.b.w = ev
            b.b.r = {}
        for b in reads:
            b.b.r[(ev[0], ev[1])] = ev

    def op(self, eng, fn, reads=(), writes=()):
        waits = self._collect(eng, reads, writes)
        idx = len(self.q[eng])
        rec = _Recorder()
        fn(rec)
        self.q[eng].append(dict(fn=rec.replay, waits=waits, sig=False, dma=None))
        self._commit(("c", eng, idx), reads, writes)

    def dma(self, queue, out, in_, reads=(), writes=(), sem=None, **kw):
        waits = self._collect(queue, reads, writes)
        sbn = sem.b.name.split(".")[0]
        cnt = self.dma_final.get(sbn, 0) + 16
        self.dma_final[sbn] = cnt
        self.dma_sems.setdefault(sbn, None)
        self.q[queue].append(dict(fn=lambda e: e.dma_start(out=out, in_=in_, **kw), waits=waits,
                                  sig=False, dma=sbn, inc=16))
        self._commit(("d", sbn, cnt), reads, writes)

    def allgather(self, in_ap, out_ap, reads=(), writes=()):
        if not hasattr(self, "ccchain"):
            self.ccchain = T(None, "ccchain")
        reads = list(reads) + [self.ccchain]
        writes = list(writes) + [self.ccchain]
        waits = self._collect("pool", reads, writes)
        cnt = self.dma_final.get("ccsem", 0) + 1
        self.dma_final["ccsem"] = cnt
        self.dma_sems.setdefault("ccsem", None)
        groups = [list(range(NCORE))]
        self.q["pool"].append(dict(fn=lambda e: e.collective_compute("AllGather", ALU.bypass, replica_groups=groups,
                                                                     ins=[in_ap], outs=[out_ap]),
                                   waits=waits, sig=False, dma="ccsem", inc=1))
        self._commit(("d", "ccsem", cnt), reads, writes)

    def wait_all_dma(self, eng):
        waits = [(("d", n), c) for n, c in self.dma_final.items()]
        self.q[eng].append(dict(fn=None, waits=waits, sig=False, dma=None))

    def finalize(self):
        nc = self.nc
        for eng, lst in self.q.items():
            for rec in lst:
                for (kind, k), v in rec["waits"]:
                    if kind == "c":
                        self.q[k][v]["sig"] = True
        for eng in COMPUTE:
            c = 0
            for rec in self.q[eng]:
                if rec["sig"]:
                    c += 1
                rec["cnt"] = c
        sems = {}
        for eng in COMPUTE:
            sems[eng] = self.stack.enter_context(nc.semaphore("c_" + eng))
        for name in self.dma_sems:
            self.dma_sems[name] = self.stack.enter_context(nc.semaphore("d_" + name))
        block = self.stack.enter_context(nc.Block())

        def emit(eng):
            def body(e):
                for rec in self.q[eng]:
                    for (kind, k), v in rec["waits"]:
                        if kind == "c":
                            e.wait_ge(sems[k], self.q[k][v]["cnt"])
                        else:
                            e.wait_ge(self.dma_sems[k], v)
                    if rec["fn"] is None:
                        continue
                    ins = rec["fn"](e)
                    if rec["dma"] is not None:
                        ins.then_inc(self.dma_sems[rec["dma"]], rec["inc"])
                    elif rec["sig"]:
                        ins.then_inc(sems[eng], 1)
            return body

        block.tensor(emit("pe"))
        block.scalar(emit("act"))
        block.vector(emit("dve"))
        block.gpsimd(emit("pool"))
        block.sync(emit("sp"))
        self.n_instr = {e: len(l) for e, l in self.q.items()}


SLOT_ELEMS = 12288


class Res:
    def __init__(self, P, nconst):
        self.P = P
        self.ps = [P.ps("ps%d" % i, [128, 512]) for i in range(8)]
        self.pss = [self.ps[0:3], self.ps[3:6]]
        self.slots = [P.sb("wslot%d" % i, [128, SLOT_ELEMS], BF16) for i in range(2)]
        self.slot_i = 0
        self.xT = P.sb("xT", [128, 32, TOK], BF16)
        self.stg = [P.sb("stg%d" % i, [128, TOK + 4], F32) for i in range(4)]
        self.stg_i = 0
        self.stb = [P.sb("stb%d" % i, [128, TOK], BF16) for i in range(2)]
        self.stb_i = 0
        self.acc = P.sb("acc", [128, TOK], F32)
        self.rstd = P.sb("rstd", [128, TOK], F32)
        self.sq = P.sb("sq", [128, TOK], F32)
        self.ones = P.sb("ones", [128, 128], F32)
        self.cst = P.sb("cst_sb", [128, nconst], F32)
        P.op("dve", lambda e: e.memset(self.ones.t[:], 1.0), writes=[self.ones])
        self.epsc = P.sb("epsc", [128, 1], F32)
        P.op("dve", lambda e: e.memset(self.epsc.t[:], EPS), writes=[self.epsc])

    def next_stg(self):
        s = self.stg[self.stg_i % len(self.stg)]
        self.stg_i += 1
        return s

    def next_stb(self):
        s = self.stb[self.stb_i % len(self.stb)]
        self.stb_i += 1
        return s

    def next_slot(self):
        s = self.slots[self.slot_i % len(self.slots)]
        self.slot_i += 1
        return s


def sumsq_begin(R):
    R.first_sq = True


def sumsq_add(R, src_ap, reads):
    P = R.P
    if R.first_sq:
        P.op("act", lambda e: e.activation(out=R.acc.t[:], in_=src_ap, func=AF.Square), reads=reads, writes=[R.acc])
        R.first_sq = False
    else:
        P.op("act", lambda e: e.activation(out=R.sq.t[:], in_=src_ap, func=AF.Square), reads=reads, writes=[R.sq])
        P.op("dve", lambda e: e.tensor_tensor(out=R.acc.t[:], in0=R.acc.t[:], in1=R.sq.t[:], op=ALU.add),
             reads=[R.sq, R.acc], writes=[R.acc])


def sumsq_finish(R, nfeat, pset):
    P = R.P
    for c, (t0, tl) in enumerate(CH):
        P.op("pe", lambda e, c=c, t0=t0, tl=tl: e.matmul(pset[c].t[:, :tl], lhsT=R.ones.t[:], rhs=R.acc.t[:, t0:t0 + tl],
                                                         start=True, stop=True),
             reads=[R.ones, R.acc], writes=[pset[c]])
        P.op("act", lambda e, c=c, t0=t0, tl=tl: e.activation(out=R.rstd.t[:, t0:t0 + tl], in_=pset[c].t[:, :tl], func=AF.Sqrt,
                                                              scale=1.0 / nfeat, bias=R.epsc.t[:, 0:1]),
             reads=[pset[c], R.epsc], writes=[R.rstd])
    P.op("dve", lambda e: e.reciprocal(out=R.rstd.t[:], in_=R.rstd.t[:]), reads=[R.rstd], writes=[R.rstd])


def apply_norm(R, src, nkt, gcol, kt_off=0):
    P = R.P
    for kt in range(nkt):
        s = R.next_stg()
        P.dma("sp", s.t[:, 0:TOK], src.t[kt * 128:(kt + 1) * 128, :], reads=[src.sub(kt)], writes=[s], sem=s)
        P.op("dve", lambda e, s=s, kt=kt: e.scalar_tensor_tensor(
            out=R.xT.t[:, kt_off + kt, :], in0=s.t[:, 0:TOK], scalar=R.cst.t[:, gcol + kt:gcol + kt + 1],
            in1=R.rstd.t[:], op0=ALU.mult, op1=ALU.mult), reads=[s, R.rstd, R.cst], writes=[R.xT])


def norm_to_xT(R, src, nkt, gcol, kt_off=0):
    P = R.P
    sumsq_begin(R)
    for kt in range(nkt):
        s = R.next_stg()
        P.dma("sp", s.t[:, 0:TOK], src.t[kt * 128:(kt + 1) * 128, :], reads=[src.sub(kt)], writes=[s], sem=s)
        sumsq_add(R, s.t[:, 0:TOK], [s])
    sumsq_finish(R, nkt * 128, R.pss[0])
    apply_norm(R, src, nkt, gcol, kt_off)


def norm_residual(R, y, h, hout, gcol, nkt=32):
    P = R.P
    sumsq_begin(R)
    for kt in range(nkt):
        a = R.next_stg()
        b = R.next_stg()
        rows = slice(kt * 128, (kt + 1) * 128)
        P.dma("sp", a.t[:, 0:TOK], y.t[rows, :], reads=[y.sub(kt)], writes=[a], sem=a)
        P.dma("sp", b.t[:, 0:TOK], h.t[rows, :], reads=[h.sub(kt)], writes=[b], sem=b)
        P.op("dve", lambda e, a=a, kt=kt: e.scalar_tensor_tensor(out=a.t[:, 0:TOK], in0=a.t[:, 0:TOK], scalar=R.cst.t[:, gcol + kt:gcol + kt + 1],
                                                                in1=R.rstd.t[:], op0=ALU.mult, op1=ALU.mult), reads=[a, R.rstd, R.cst], writes=[a])
        P.op("dve", lambda e, a=a, b=b: e.tensor_tensor(out=b.t[:, 0:TOK], in0=a.t[:, 0:TOK], in1=b.t[:, 0:TOK], op=ALU.add),
             reads=[a, b], writes=[b])
        P.dma("act", hout.t[rows, :], b.t[:, 0:TOK], reads=[b], writes=[hout.sub(kt)], sem=b)
        sumsq_add(R, b.t[:, 0:TOK], [b])
    sumsq_finish(R, nkt * 128, R.pss[0])


def gemm(R, KT, panels, epi, xT=None, kt0=0):
    P = R.P
    xT = xT or R.xT
    ti = 0
    for panel in panels:
        slot = R.next_slot()
        pw = sum(t[2] for t in panel)
        assert KT * pw <= SLOT_ELEMS, (KT, pw)
        view = slot.t[:, 0:KT * pw].rearrange("p (k n) -> p k n", n=pw)
        runs = []
        off = 0
        for (w, c0, wd, key) in panel:
            if runs and runs[-1][0] is w and runs[-1][1] + runs[-1][2] == c0:
                runs[-1][2] += wd
            else:
                runs.append([w, c0, wd, off])
            off += wd
        for (w, c0, wd, o) in runs:
            kstep = 8
            for k0 in range(0, KT, kstep):
                k1 = min(KT, k0 + kstep)
                src = w.t[k0 * 128:k1 * 128, c0:c0 + wd].rearrange("(k p) n -> p k n", p=128)
                P.dma("pool", view[:, k0:k1, o:o + wd], src, reads=[w], writes=[slot], sem=slot)
        off = 0
        for (w, c0, wd, key) in panel:
            pset = R.pss[ti % 2]
            ti += 1
            for kt in range(KT):
                for c, (t0, tl) in enumerate(CH):
                    P.op("pe", lambda e, c=c, t0=t0, tl=tl, kt=kt, off=off, wd=wd, pset=pset, view=view: e.matmul(
                        pset[c].t[:wd, :tl], lhsT=view[:, kt, off:off + wd], rhs=xT.t[:, kt0 + kt, t0:t0 + tl],
                        start=(kt == 0), stop=(kt == KT - 1)), reads=[slot, xT], writes=[pset[c]])
            epi(key, wd, pset)
            off += wd


class Weights:
    def __init__(self, P, specs, srcs):
        self.src = {n: P.dram(n, list(shp), F32, kind="ExternalInput") for n, shp in srcs.items()}
        self.pieces = {}
        for (key, src, r0, nr, c0, ncl) in specs:
            self.pieces[key] = T(self.src[src].t[r0:r0 + nr, c0:c0 + ncl], "%s_%s" % (src, key))

    def __getitem__(self, key):
        return self.pieces[key]


def pack_shards(specs, arrays):
    out = []
    for r in range(NCORE):
        parts = []
        for (key, nr, ncl) in specs:
            a = arrays[key]
            assert a.shape == (nr, ncl), (key, a.shape, nr, ncl)
            parts.append(np.ascontiguousarray(a[r * (nr // 8):(r + 1) * (nr // 8)]).reshape(-1))
        out.append(np.concatenate(parts).astype(np.float32, copy=False))
    return out


def evac(R, pset, wd, dst_ap_fn, dst, eng="act"):
    P = R.P
    for c, (t0, tl) in enumerate(CH):
        if eng == "act":
            P.op("act", lambda e, c=c, t0=t0, tl=tl: e.activation(out=dst_ap_fn(t0, tl), in_=pset[c].t[:wd, :tl], func=AF.Copy),
                 reads=[pset[c]], writes=[dst])
        else:
            P.op("dve", lambda e, c=c, t0=t0, tl=tl: e.tensor_copy(out=dst_ap_fn(t0, tl), in_=pset[c].t[:wd, :tl]),
                 reads=[pset[c]], writes=[dst])


def epi_store(R, dst, row0, bf=False, eng="act"):
    P = R.P

    def f(c0, wd, pset, r0):
        s = R.next_stb() if bf else R.next_stg()
        evac(R, pset, wd, lambda t0, tl: s.t[:wd, t0:t0 + tl], s, eng=eng)
        P.dma("act", dst.t[r0:r0 + wd, :], s.t[:wd, 0:TOK], reads=[s], writes=[dst.sub(r0 // 128)], sem=s)
    return f


def run_prog(build_fn, in_maps, trace=False):
    nc = bass.Bass("TRN2", target_bir_lowering=False)
    with contextlib.ExitStack() as stack:
        P = Prog(nc, stack)
        build_fn(nc, P)
        P.wait_all_dma("sp")
        P.finalize()
    res = run_bass_kernel_spmd(nc, in_maps, core_ids=list(range(NCORE)), trace=trace)
    return res, P


A0_OF_ROWS = 6144
A0_OB_ROWS = 9280
A0_NCONST = 32 + 8 + 4
A0_WSPECS = [("in%d" % i, "w_in", 0, D, 2304 * i, 2304) for i in range(4)] + [("in4", "w_in", 0, D, 9216, 576),
                                                                             ("uq", "w_uq", 0, 1024, 0, 3072), ("ukv", "w_ukv", 0, 512, 0, 4096)]
A0_WSRCS = {"w_in": (D, 9792), "w_uq": (1024, 3072), "w_ukv": (512, 4096)}


def rope_epi(R, P, ps1, ps2, wd, cosT, sinT, dst, r1, r2):
    t1 = R.next_stg()
    t2 = R.next_stg()
    o1 = R.next_stb()
    o2 = R.next_stb()
    for c, (t0, tl) in enumerate(CH):
        sl = slice(t0, t0 + tl)
        P.op("dve", lambda e, c=c, sl=sl, tl=tl: e.tensor_tensor(out=t1.t[:wd, sl], in0=ps1[c].t[:wd, :tl], in1=cosT.t[:wd, sl], op=ALU.mult),
             reads=[ps1[c], cosT], writes=[t1])
        P.op("dve", lambda e, c=c, sl=sl, tl=tl: e.tensor_tensor(out=t2.t[:wd, sl], in0=ps2[c].t[:wd, :tl], in1=sinT.t[:wd, sl], op=ALU.mult),
             reads=[ps2[c], sinT], writes=[t2])
    P.op("dve", lambda e: e.tensor_tensor(out=o1.t[:wd, :], in0=t1.t[:wd, 0:TOK], in1=t2.t[:wd, 0:TOK], op=ALU.subtract),
         reads=[t1, t2], writes=[o1])
    P.dma("act", dst.t[r1:r1 + wd, :], o1.t[:wd, :], reads=[o1], writes=[dst.sub(("r", r1))], sem=o1)
    t3 = R.next_stg()
    t4 = R.next_stg()
    for c, (t0, tl) in enumerate(CH):
        sl = slice(t0, t0 + tl)
        P.op("dve", lambda e, c=c, sl=sl, tl=tl: e.tensor_tensor(out=t3.t[:wd, sl], in0=ps1[c].t[:wd, :tl], in1=sinT.t[:wd, sl], op=ALU.mult),
             reads=[ps1[c], sinT], writes=[t3])
        P.op("dve", lambda e, c=c, sl=sl, tl=tl: e.tensor_tensor(out=t4.t[:wd, sl], in0=ps2[c].t[:wd, :tl], in1=cosT.t[:wd, sl], op=ALU.mult),
             reads=[ps2[c], cosT], writes=[t4])
    P.op("dve", lambda e: e.tensor_tensor(out=o2.t[:wd, :], in0=t3.t[:wd, 0:TOK], in1=t4.t[:wd, 0:TOK], op=ALU.add),
         reads=[t3, t4], writes=[o2])
    P.dma("act", dst.t[r2:r2 + wd, :], o2.t[:wd, :], reads=[o2], writes=[dst.sub(("r", r2))], sem=o2)


def build_A0(nc, P):
    hT = P.dram("hT", [D, TOK], F32, kind="ExternalInput")
    cst_d = P.dram("cst", [128, A0_NCONST], F32, kind="ExternalInput")
    rope_d = P.dram("rope", [2, 128, TOK], F32, kind="ExternalInput")
    W = Weights(P, A0_WSPECS, A0_WSRCS)
    oF = P.dram("oF", [A0_OF_ROWS, TOK], F32, kind="ExternalOutput")
    oB = P.dram("oB", [A0_OB_ROWS, TOK], BF16, kind="ExternalOutput")
    scr = P.dram("scrA0", [1536, TOK], F32)
    R = Res(P, A0_NCONST)
    cosT = P.sb("cosT", [128, TOK])
    sinT = P.sb("sinT", [128, TOK])
    P.dma("sp", R.cst.t[:], cst_d.t[:, :], writes=[R.cst], sem=R.cst)
    P.dma("sp", cosT.t[:], rope_d.t[0], writes=[cosT], sem=cosT)
    P.dma("sp", sinT.t[:], rope_d.t[1], writes=[sinT], sem=sinT)

    norm_to_xT(R, hT, 32, 0)

    st_f = epi_store(R, oF, 0)
    st_b = epi_store(R, oB, 0, bf=True)
    st_s = epi_store(R, scr, 0)
    held = {}

    def epi_in(c0, wd, pset):
        if c0 < 4096:
            st_f(c0, wd, pset, c0)
        elif c0 < 6144:
            st_b(c0, wd, pset, c0 - 4096)
        elif c0 < 8192:
            st_f(c0, wd, pset, c0 - 6144 + 4096)
        elif c0 < 9728:
            st_s(c0, wd, pset, c0 - 8192)
        elif c0 == 9728:
            held["kr1"] = pset
        else:
            rope_epi(R, P, held["kr1"], pset, 32, cosT, sinT, oB, 9216, 9248)

    tiles = [(128 * i, 128) for i in range(76)] + [(9728, 32), (9760, 32)]
    tiles = [(W["in%d" % min(c0 // 2304, 4)], c0 - 2304 * min(c0 // 2304, 4), wd, c0) for (c0, wd) in tiles]
    panels = [tiles[i:i + 3] for i in range(0, 75, 3)] + [tiles[75:78]]
    gemm(R, 32, panels, epi_in)

    norm_to_xT(R, scr, 8, 32)

    def epi_uq(c0, wd, pset):
        if c0 < 2048:
            st_b(c0, wd, pset, 2048 + c0)
        elif c0 < 2560:
            held["x1"] = (pset, c0)
        else:
            j = (c0 - 2560) // 128
            rope_epi(R, P, held["x1"][0], pset, 128, cosT, sinT, oB, 4096 + 128 * j, 4608 + 128 * j)

    nope = [(W["uq"], 128 * h, 128, 128 * h) for h in range(16)]
    ropet = []
    for j in range(4):
        ropet += [(W["uq"], 2048 + 128 * j, 128, 2048 + 128 * j), (W["uq"], 2560 + 128 * j, 128, 2560 + 128 * j)]
    gemm(R, 8, [nope[0:12], nope[12:16], ropet], epi_uq)

    scr_kv = T(scr.t[1024:1536, :], "scrA0")
    scr_kv.subs = {k: scr.sub(8 + k) for k in range(4)}
    norm_to_xT(R, scr_kv, 4, 40)

    def epi_kv(c0, wd, pset):
        st_b(c0, wd, pset, 5120 + c0)

    kvt = [(W["ukv"], 128 * i, 128, 128 * i) for i in range(32)]
    gemm(R, 4, [kvt[0:24], kvt[24:32]], epi_kv)


def rope_tables_np(pos):
    inv = (10000.0 ** (-np.arange(0, 64, 2, dtype=np.float32) / np.float32(64))).astype(np.float32)
    ang = pos.astype(np.float32)[:, None] * inv[None, :]
    return np.cos(ang).astype(np.float32), np.sin(ang).astype(np.float32)


def col128(v, n):
    return np.ascontiguousarray(v.reshape(n, 128).T)


def host_A0(inp):
    x = inp["x"][0]
    h = np.concatenate([inp["meta_tokens"], x], axis=0)
    w_uq = inp["mla_w_uq"][0]
    idx = []
    for hh in range(16):
        idx += list(range(192 * hh, 192 * hh + 128))
    for half in range(2):
        for hh in range(16):
            idx += list(range(192 * hh + 128 + 32 * half, 192 * hh + 160 + 32 * half))
    w_uq_p = np.ascontiguousarray(w_uq[:, idx])
    cst = np.concatenate([col128(inp["norm_g"][0, 0], 32), col128(inp["mla_q_norm_g"][0], 8),
                          col128(inp["mla_kv_norm_g"][0], 4)], axis=1).astype(np.float32)
    maps = []
    for c in range(NCORE):
        hT = np.ascontiguousarray(h[1024 * c:1024 * c + TOK].T)
        cos, sin = rope_tables_np(np.arange(1024 * c, 1024 * c + TOK))
        rope = np.stack([np.tile(cos.T, (4, 1)), np.tile(sin.T, (4, 1))]).astype(np.float32)
        maps.append({"hT": hT, "cst": cst, "rope": rope, "w_in": inp["ab_w_in"][0], "w_uq": w_uq_p, "w_ukv": inp["mla_w_ukv"][0]})
    return maps


GC = 8
SCALE = 192 ** -0.5


def bcast_mid(ap2d, n):
    return ap2d.unsqueeze(2).broadcast_to([ap2d.shape[0], ap2d.shape[1], n])


def build_B0(nc, P, LPv=LP, do_hgrn=True, do_attn=True):
    NCH = LPv // 64
    hq = P.dram("hq", [2, 128, LPv], F32, kind="ExternalInput")
    hf = P.dram("hf", [2, 128, LPv], F32, kind="ExternalInput")
    hv = P.dram("hv", [2, 128, LPv], BF16, kind="ExternalInput")
    lbl = P.dram("lbl", [128, 2, 3], F32, kind="ExternalInput")
    aq = P.dram("aq", [2, 128, LPv], BF16, kind="ExternalInput")
    aqr = P.dram("aqr", [2, 64, LPv], BF16, kind="ExternalInput")
    ak = P.dram("ak", [2, 128, LPv], BF16, kind="ExternalInput")
    akr = P.dram("akr", [64, LPv], BF16, kind="ExternalInput")
    av = P.dram("av", [2, 128, LPv], BF16, kind="ExternalInput")
    cbf = P.dram("cbf", [128, 128 + 4 * 512 + 128 + 128], BF16, kind="ExternalInput")
    cf = P.dram("cf", [128, 64 + 1664], F32, kind="ExternalInput")
    oa = P.dram("oa", [2, 128, LPv], F32, kind="ExternalOutput")
    ob = P.dram("ob", [2, 128, LPv], BF16, kind="ExternalOutput")

    ps = [P.ps("ps%d" % i, [128, 512]) for i in range(8)]
    cb = P.sb("cb_sb", [128, 128 + 4 * 512 + 256], BF16)
    cfs = P.sb("cf_sb", [128, 64 + 1664], F32)
    P.dma("sp", cb.t[:], cbf.t[:, :], writes=[cb], sem=cb)
    P.dma("sp", cfs.t[:], cf.t[:, :], writes=[cfs], sem=cfs)
    ident = cb.t[:, 0:128]
    masks = cb.t[:, 128:128 + 2048].rearrange("p (r q) -> p r q", q=512)
    onesb = cb.t[:, 2176:2304]
    ones0 = cb.t[:, 2304:2432]
    triu = cfs.t[0:64, 0:64]

    if do_hgrn:
        BL = 1664 if LPv % 1664 == 0 else LPv
        NBLK = LPv // BL
        CPB = BL // 64
        rmask = cfs.t[:, 64:64 + BL]
        lb3 = P.sb("lb3", [128, 2, 3])
        lbe = P.sb("lbe", [128, 2, 3])
        lbs = P.sb("lbs", [128, 2])
        lb = P.sb("lb", [128, 2])
        oml = P.sb("oml", [128, 2])
        P.dma("sp", lb3.t[:], lbl.t[:, :, :], writes=[lb3], sem=lb3)
        P.op("act", lambda e: e.activation(out=lbe.t[:], in_=lb3.t[:], func=AF.Exp), reads=[lb3], writes=[lbe])
        P.op("dve", lambda e: e.tensor_tensor(out=lbs.t[:], in0=lbe.t[:, :, 0], in1=lbe.t[:, :, 1], op=ALU.add), reads=[lbe], writes=[lbs])
        P.op("dve", lambda e: e.tensor_tensor(out=lbs.t[:], in0=lbs.t[:], in1=lbe.t[:, :, 2], op=ALU.add), reads=[lbe, lbs], writes=[lbs])
        P.op("dve", lambda e: e.reciprocal(out=lbs.t[:], in_=lbs.t[:]), reads=[lbs], writes=[lbs])
        P.op("dve", lambda e: e.tensor_tensor(out=lb.t[:], in0=lbe.t[:, :, 0], in1=lbs.t[:], op=ALU.mult), reads=[lbe, lbs], writes=[lb])
        P.op("dve", lambda e: e.tensor_scalar(out=oml.t[:], in0=lb.t[:], scalar1=-1.0, scalar2=1.0, op0=ALU.mult, op1=ALU.add),
             reads=[lb], writes=[oml])

        def fb(name, dt=F32, n=2):
            return [P.sb("%s%d" % (name, i), [128, BL], dt) for i in range(n)]
        qin, fin, vin = fb("qin"), fb("fin"), fb("vin", BF16)
        kk, bb, eb, enb = fb("kk", n=1)[0], fb("bb", n=1)[0], fb("eb"), fb("enb", n=1)[0]
        qt, ktb, kh = fb("qt", BF16), fb("ktb", BF16), fb("kh", BF16)
        obuf = fb("obuf")
        vtok = [P.sb("vtok%d" % i, [64, GC, 128], BF16) for i in range(2)]
        khtok = [P.sb("khtok%d" % i, [64, GC, 128], BF16) for i in range(2)]
        atm = [P.sb("atm%d" % i, [64, GC, 64], BF16) for i in range(2)]
        S = P.sb("S", [128, 128])
        Sb = P.sb("Sb", [128, 128], BF16)
        it = 0
        gi = 0
        for hh in range(2):
            P.op("dve", lambda e: e.memset(S.t[:], 0.0), writes=[S])
            P.op("dve", lambda e: e.memset(Sb.t[:], 0.0), writes=[Sb])
            for blk in range(NBLK):
                i2 = it % 2
                it += 1
                sl = slice(blk * BL, (blk + 1) * BL)
                q_, f_, v_, eb_, qt_, ktb_, kh_, ob_ = qin[i2], fin[i2], vin[i2], eb[i2], qt[i2], ktb[i2], kh[i2], obuf[i2]
                P.dma("sp", q_.t[:], hq.t[hh, :, sl], writes=[q_], sem=q_)
                P.dma("sp", f_.t[:], hf.t[hh, :, sl], writes=[f_], sem=f_)
                P.dma("sp", v_.t[:], hv.t[hh, :, sl], writes=[v_], sem=v_)
                P.op("act", lambda e, f_=f_: e.activation(out=f_.t[:], in_=f_.t[:], func=AF.Sigmoid), reads=[f_], writes=[f_])
                P.op("dve", lambda e, f_=f_, hh=hh: e.tensor_scalar(out=f_.t[:], in0=f_.t[:], scalar1=oml.t[:, hh:hh + 1], scalar2=lb.t[:, hh:hh + 1],
                                                                    op0=ALU.mult, op1=ALU.add), reads=[f_, oml, lb], writes=[f_])
                P.op("dve", lambda e, f_=f_: e.tensor_scalar(out=kk.t[:], in0=f_.t[:], scalar1=-1.0, scalar2=1.0, op0=ALU.mult, op1=ALU.add),
                     reads=[f_], writes=[kk])
                P.op("act", lambda e, f_=f_: e.activation(out=f_.t[:], in_=f_.t[:], func=AF.Ln), reads=[f_], writes=[f_])
                P.op("dve", lambda e, f_=f_: e.tensor_tensor_scan(out=bb.t[:], data0=rmask, data1=f_.t[:], initial=0.0, op0=ALU.mult, op1=ALU.add),
                     reads=[f_, cfs], writes=[bb])
                P.op("act", lambda e, eb_=eb_: e.activation(out=eb_.t[:], in_=bb.t[:], func=AF.Exp), reads=[bb], writes=[eb_])
                P.op("act", lambda e: e.activation(out=enb.t[:], in_=bb.t[:], func=AF.Exp, scale=-1.0), reads=[bb], writes=[enb])
                P.op("dve", lambda e, q_=q_, eb_=eb_, qt_=qt_: e.tensor_tensor(out=qt_.t[:], in0=q_.t[:], in1=eb_.t[:], op=ALU.mult),
                     reads=[q_, eb_], writes=[qt_])
                P.op("dve", lambda e: e.tensor_tensor(out=kk.t[:], in0=kk.t[:], in1=enb.t[:], op=ALU.mult), reads=[kk, enb], writes=[kk])
                P.op("act", lambda e, ktb_=ktb_: e.activation(out=ktb_.t[:], in_=kk.t[:], func=AF.Copy), reads=[kk], writes=[ktb_])
                P.op("dve", lambda e, eb_=eb_, kh_=kh_: e.tensor_tensor(
                    out=kh_.t[:].rearrange("p (c t) -> p c t", t=64), in0=kk.t[:].rearrange("p (c t) -> p c t", t=64),
                    in1=bcast_mid(eb_.t[:, 63::64], 64), op=ALU.mult), reads=[kk, eb_], writes=[kh_])
                for g0 in range(0, CPB, GC):
                    gn = min(GC, CPB - g0)
                    g2 = gi % 2
                    gi += 1
                    vt_, kt_, at_ = vtok[g2], khtok[g2], atm[g2]
                    pv, pk, pa, po = ps[0 + g2], ps[2 + g2], ps[4], ps[5]
                    for c in range(gn):
                        cs = slice((g0 + c) * 64, (g0 + c + 1) * 64)
                        half = c // 4
                        P.op("pe", lambda e, c=c, cs=cs, v_=v_, pv=pv, pk=pk: e.matmul(
                            (pv if c < 4 else pk).t[0:64, (c % 4) * 128:(c % 4 + 1) * 128], lhsT=v_.t[:, cs], rhs=ident, start=True, stop=True),
                            reads=[v_, cb], writes=[pv if c < 4 else pk])
                    for half in range((gn + 3) // 4):
                        n4 = min(4, gn - half * 4)
                        src = (pv if half == 0 else pk)
                        P.op("act", lambda e, half=half, n4=n4, src=src, vt_=vt_: e.activation(
                            out=vt_.t[:, half * 4:half * 4 + n4, :], in_=src.t[0:64, 0:n4 * 128].rearrange("p (c d) -> p c d", d=128), func=AF.Copy),
                            reads=[src], writes=[vt_])
                    for c in range(gn):
                        cs = slice((g0 + c) * 64, (g0 + c + 1) * 64)
                        P.op("pe", lambda e, c=c, cs=cs, kh_=kh_, pv=pv, pk=pk: e.matmul(
                            (pv if c < 4 else pk).t[0:64, (c % 4) * 128:(c % 4 + 1) * 128], lhsT=kh_.t[:, cs], rhs=ident, start=True, stop=True),
                            reads=[kh_, cb], writes=[pv if c < 4 else pk])
                    for half in range((gn + 3) // 4):
                        n4 = min(4, gn - half * 4)
                        src = (pv if half == 0 else pk)
                        P.op("dve", lambda e, half=half, n4=n4, src=src, kt_=kt_: e.tensor_copy(
                            out=kt_.t[:, half * 4:half * 4 + n4, :], in_=src.t[0:64, 0:n4 * 128].rearrange("p (c d) -> p c d", d=128)),
                            reads=[src], writes=[kt_])
                    for c in range(gn):
                        cs = slice((g0 + c) * 64, (g0 + c + 1) * 64)
                        P.op("pe", lambda e, c=c, cs=cs, ktb_=ktb_, qt_=qt_: e.matmul(pa.t[0:64, c * 64:(c + 1) * 64], lhsT=ktb_.t[:, cs], rhs=qt_.t[:, cs],
                                                                                     start=True, stop=True), reads=[ktb_, qt_], writes=[pa])
                    P.op("dve", lambda e, gn=gn, at_=at_: e.tensor_tensor(
                        out=at_.t[:, 0:gn, :], in0=pa.t[0:64, 0:gn * 64].rearrange("p (c t) -> p c t", t=64),
                        in1=triu.unsqueeze(1).broadcast_to([64, gn, 64]), op=ALU.mult), reads=[pa, cfs], writes=[at_])
                    for c in range(gn):
                        cg = g0 + c
                        cs = slice(cg * 64, (cg + 1) * 64)
                        pn = ps[6 + (cg % 2)]
                        P.op("pe", lambda e, c=c, cs=cs, qt_=qt_: e.matmul(po.t[:, c * 64:(c + 1) * 64], lhsT=Sb.t[:], rhs=qt_.t[:, cs], start=True, stop=False),
                             reads=[Sb, qt_], writes=[po])
                        P.op("pe", lambda e, c=c, vt_=vt_, at_=at_: e.matmul(po.t[:, c * 64:(c + 1) * 64], lhsT=vt_.t[:, c, :], rhs=at_.t[:, c, :], start=False, stop=True),
                             reads=[vt_, at_], writes=[po])
                        P.op("pe", lambda e, c=c, kt_=kt_, vt_=vt_, pn=pn: e.matmul(pn.t[:, 0:128], lhsT=kt_.t[:, c, :], rhs=vt_.t[:, c, :], start=True, stop=True),
                             reads=[kt_, vt_], writes=[pn])
                        P.op("dve", lambda e, cg=cg, eb_=eb_, pn=pn: e.scalar_tensor_tensor(out=S.t[:], in0=S.t[:], scalar=eb_.t[:, cg * 64 + 63:cg * 64 + 64],
                                                                                          in1=pn.t[:, 0:128], op0=ALU.mult, op1=ALU.add),
                             reads=[S, eb_, pn], writes=[S])
                        P.op("act", lambda e: e.activation(out=Sb.t[:], in_=S.t[:], func=AF.Copy), reads=[S], writes=[Sb])
                    P.op("act", lambda e, g0=g0, gn=gn, ob_=ob_: e.activation(out=ob_.t[:, g0 * 64:(g0 + gn) * 64], in_=po.t[:, 0:gn * 64], func=AF.Copy),
                         reads=[po], writes=[ob_])
                P.dma("sp", oa.t[hh, :, sl], ob_.t[:], reads=[ob_], writes=[oa.sub((hh, blk))], sem=ob_)

    if do_attn:
        NKT = LPv // 128
        kn = P.sb("kn", [128, LPv], BF16)
        krr = P.sb("krr", [64, LPv], BF16)
        qnb = [P.sb("qn%d" % i, [128, 512], BF16) for i in range(2)]
        qrb = [P.sb("qr%d" % i, [64, 512], BF16) for i in range(2)]
        vT = P.sb("vT", [128, LPv], BF16)
        vk = P.sb("vk", [128, NKT, 128], BF16)
        pT = [P.sb("pT%d" % i, [128, 512], BF16) for i in range(3)]
        ost = [P.sb("ost%d" % i, [128, 512], BF16) for i in range(2)]
        rden = P.sb("rden", [128, 512])
        P.dma("sp", krr.t[:], akr.t[:, :], writes=[krr], sem=krr)
        pi = 0
        oi = 0
        for hh in range(2):
            P.dma("sp", kn.t[:], ak.t[hh], writes=[kn], sem=kn)
            P.dma("sp", vT.t[:], av.t[hh], writes=[vT], sem=vT)
            for j0 in range(0, NKT, 4):
                jn = min(4, NKT - j0)
                pv = ps[(j0 // 4) % 2]
                for j in range(jn):
                    P.op("pe", lambda e, j=j, j0=j0, pv=pv: e.matmul(pv.t[:, j * 128:(j + 1) * 128], lhsT=vT.t[:, (j0 + j) * 128:(j0 + j + 1) * 128],
                                                                   rhs=ident, start=True, stop=True), reads=[vT, cb], writes=[pv])
                P.op("dve", lambda e, j0=j0, jn=jn, pv=pv: e.tensor_copy(out=vk.t[:, j0:j0 + jn, :],
                                                                       in_=pv.t[:, 0:jn * 128].rearrange("p (c d) -> p c d", d=128)),
                     reads=[pv], writes=[vk])
            nq = (LPv + 511) // 512
            for qi in range(nq):
                q0 = qi * 512
                qw = min(512, LPv - q0)
                jmax = min(NKT - 1, 4 * qi + 3)
                po, pd = ps[4 + 2 * (qi % 2)], ps[5 + 2 * (qi % 2)]
                qn, qr = qnb[qi % 2], qrb[qi % 2]
                P.dma("sp", qn.t[:, 0:qw], aq.t[hh, :, q0:q0 + qw], writes=[qn], sem=qn)
                P.dma("sp", qr.t[:, 0:qw], aqr.t[hh, :, q0:q0 + qw], writes=[qr], sem=qr)
                for j in range(jmax + 1):
                    pss_ = ps[j % 2 + 2] if False else ps[(j % 2)]
                    ks = slice(j * 128, (j + 1) * 128)
                    P.op("pe", lambda e, ks=ks, q0=q0, qw=qw, pss_=pss_: e.matmul(pss_.t[:, 0:qw], lhsT=kn.t[:, ks], rhs=qn.t[:, 0:qw], start=True, stop=False),
                         reads=[kn, qn], writes=[pss_])
                    P.op("pe", lambda e, ks=ks, q0=q0, qw=qw, pss_=pss_: e.matmul(pss_.t[:, 0:qw], lhsT=krr.t[:, ks], rhs=qr.t[:, 0:qw], start=False, stop=True),
                         reads=[krr, qr], writes=[pss_])
                    p_ = pT[pi % 3]
                    pi += 1
                    P.op("act", lambda e, qw=qw, pss_=pss_, p_=p_: e.activation(out=p_.t[:, 0:qw], in_=pss_.t[:, 0:qw], func=AF.Exp, scale=SCALE),
                         reads=[pss_], writes=[p_])
                    r = j - 4 * qi
                    if r >= 0:
                        P.op("dve", lambda e, qw=qw, r=r, p_=p_: e.tensor_tensor(out=p_.t[:, 0:qw], in0=p_.t[:, 0:qw], in1=masks[:, r, 0:qw], op=ALU.mult),
                             reads=[p_, cb], writes=[p_])
                    P.op("pe", lambda e, j=j, qw=qw, p_=p_, jmax=jmax, po=po: e.matmul(po.t[:, 0:qw], lhsT=vk.t[:, j, :], rhs=p_.t[:, 0:qw], start=(j == 0), stop=(j == jmax)),
                         reads=[vk, p_], writes=[po])
                    P.op("pe", lambda e, j=j, qw=qw, p_=p_, jmax=jmax, pd=pd: e.matmul(pd.t[:, 0:qw], lhsT=(ones0 if j == 0 else onesb), rhs=p_.t[:, 0:qw], start=(j == 0), stop=(j == jmax)),
                         reads=[cb, p_], writes=[pd])
                o_ = ost[oi % 2]
                oi += 1
                P.op("dve", lambda e, qw=qw, pd=pd: e.tensor_scalar(out=rden.t[:, 0:qw], in0=pd.t[:, 0:qw], scalar1=1e-30, scalar2=None, op0=ALU.max),
                     reads=[pd], writes=[rden])
                P.op("dve", lambda e, qw=qw: e.reciprocal(out=rden.t[:, 0:qw], in_=rden.t[:, 0:qw]), reads=[rden], writes=[rden])
                P.op("dve", lambda e, qw=qw, po=po, o_=o_: e.tensor_tensor(out=o_.t[:, 0:qw], in0=po.t[:, 0:qw], in1=rden.t[:, 0:qw], op=ALU.mult),
                     reads=[po, rden], writes=[o_])
                P.dma("sp", ob.t[hh, :, q0:q0 + qw], o_.t[:, 0:qw], reads=[o_], writes=[ob.sub((hh, qi))], sem=o_)


def consts_B0():
    cbf = np.zeros((128, 128 + 2048 + 256), np.float32)
    cbf[:, 0:128] = np.eye(128)
    k = np.arange(128)[:, None]
    q = np.arange(512)[None, :]
    for r in range(4):
        cbf[:, 128 + 512 * r:128 + 512 * (r + 1)] = ((2 * r + (k >= 64)) <= (q // 64)).astype(np.float32)
    cbf[:, 2176:2304] = 1.0
    cbf[PADL:, 2304:2432] = 1.0
    cf = np.zeros((128, 64 + 1664), np.float32)
    s = np.arange(64)[:, None]
    t = np.arange(64)[None, :]
    cf[0:64, 0:64] = (s <= t).astype(np.float32)
    rm = np.ones(1664, np.float32)
    rm[0::64] = 0.0
    cf[:, 64:] = rm[None, :]
    return cbf.astype(NPBF), cf


def assemble(outs):
    rows = outs[0].shape[0]
    full = np.zeros((rows, LP), outs[0].dtype)
    full[:, PADL:PADL + TOK] = outs[0]
    for c in range(1, NCORE):
        full[:, PADL + 1024 * c + 16:PADL + 1024 * c + TOK] = outs[c][:, 16:]
    return full


def host_B0(inp, oF, oB, LPv=LP):
    fF = assemble(oF)[:, :LPv]
    fB = assemble(oB)[:, :LPv]
    cbf, cf = consts_B0()
    lbl = inp["lb_logits"]
    maps = []
    for c in range(NCORE):
        hs = [2 * c, 2 * c + 1]
        m = {}
        m["hq"] = np.stack([fF[128 * h:128 * h + 128] for h in hs])
        m["hf"] = np.stack([fF[2048 + 128 * h:2048 + 128 * h + 128] for h in hs])
        m["hv"] = np.stack([fB[128 * h:128 * h + 128] for h in hs])
        m["lbl"] = np.ascontiguousarray(np.stack([lbl[:, 128 * h:128 * h + 128].T for h in hs], axis=1))
        m["aq"] = np.stack([fB[2048 + 128 * h:2048 + 128 * h + 128] for h in hs])
        qr = []
        for h in hs:
            j, hh = h // 4, h % 4
            qr.append(np.concatenate([fB[4096 + 128 * j + 32 * hh:4096 + 128 * j + 32 * hh + 32],
                                      fB[4608 + 128 * j + 32 * hh:4608 + 128 * j + 32 * hh + 32]], axis=0))
        m["aqr"] = np.stack(qr)
        m["ak"] = np.stack([fB[5120 + 256 * h:5120 + 256 * h + 128] for h in hs])
        m["av"] = np.stack([fB[5120 + 256 * h + 128:5120 + 256 * h + 256] for h in hs])
        m["akr"] = np.ascontiguousarray(fB[9216:9280])
        m["cbf"] = cbf
        m["cf"] = cf
        maps.append({k: np.ascontiguousarray(v) for k, v in m.items()})
    return maps


FFN_G = [15, 15, 14, 14, 14, 14]
NT_UP = 172

C_G1, C_G2, C_G3, C_GN, C_CW, C_CB, C_X = 0, 32, 64, 96, 128, 128 + 516, 128 + 516 + 172
C_NBASE = C_X


def ffn(R, P, W, ffT, aT, U, Y, Gt):
    c0g = 0
    for gidx, gsz in enumerate(FFN_G):
        held = {}

        def conv(pset, tile, ui):
            u = U[ui % 2]
            y = Y[ui % 2]
            for c, (t0, tl) in enumerate(CH):
                P.op("act", lambda e, c=c, t0=t0, tl=tl: e.activation(out=u.t[:, 2 + t0:2 + t0 + tl], in_=pset[c].t[:, :tl], func=AF.Copy),
                     reads=[pset[c]], writes=[u])
            cw = C_CW + 3 * tile
            P.op("act", lambda e: e.activation(out=y.t[:], in_=u.t[:, 2:2 + TOK], func=AF.Identity,
                                               scale=R.cst.t[:, cw + 2:cw + 3], bias=R.cst.t[:, C_CB + tile:C_CB + tile + 1]),
                 reads=[u, R.cst], writes=[y])
            P.op("dve", lambda e: e.scalar_tensor_tensor(out=y.t[:], in0=u.t[:, 1:1 + TOK], scalar=R.cst.t[:, cw + 1:cw + 2], in1=y.t[:],
                                                         op0=ALU.mult, op1=ALU.add), reads=[u, y, R.cst], writes=[y])
            P.op("dve", lambda e: e.scalar_tensor_tensor(out=y.t[:], in0=u.t[:, 0:TOK], scalar=R.cst.t[:, cw:cw + 1], in1=y.t[:],
                                                         op0=ALU.mult, op1=ALU.add), reads=[u, y, R.cst], writes=[y])
            return y

        ui = [0]

        def epi_up(c0, wd, pset):
            tile = c0 // 128
            y = conv(pset, tile, ui[0])
            ui[0] += 1
            if c0 < DFF:
                P.op("act", lambda e: e.activation(out=Gt.t[:], in_=y.t[:], func=AF.Silu), reads=[y], writes=[Gt])
            else:
                jj = (c0 - DFF) // 128 - c0g
                P.op("dve", lambda e: e.tensor_tensor(out=aT.t[:, jj, :], in0=Gt.t[:], in1=y.t[:], op=ALU.mult), reads=[Gt, y], writes=[aT])

        panels = [[(W["upg%d" % gidx], 128 * (j - c0g), 128, 128 * j), (W["upv%d" % gidx], 128 * (j - c0g), 128, DFF + 128 * j)]
                  for j in range(c0g, c0g + gsz)]
        gemm(R, 32, panels, epi_up)

        last = gidx == len(FFN_G) - 1
        if last:
            sumsq_begin(R)

        def epi_dn(c0, wd, pset):
            s = R.next_stg()
            kt = c0 // 128
            if gidx == 0:
                evac(R, pset, wd, lambda t0, tl: s.t[:wd, t0:t0 + tl], s)
            else:
                p = R.next_stg()
                P.dma("sp", p.t[:, 0:TOK], ffT.t[c0:c0 + 128, :], reads=[ffT.sub(kt)], writes=[p], sem=p)
                for c, (t0, tl) in enumerate(CH):
                    P.op("dve", lambda e, c=c, t0=t0, tl=tl: e.tensor_tensor(out=s.t[:, t0:t0 + tl], in0=pset[c].t[:, :tl], in1=p.t[:, t0:t0 + tl], op=ALU.add),
                         reads=[pset[c], p], writes=[s])
            P.dma("act", ffT.t[c0:c0 + 128, :], s.t[:, 0:TOK], reads=[s], writes=[ffT.sub(kt)], sem=s)
            if last:
                sumsq_add(R, s.t[:, 0:TOK], [s])

        tiles = [(W["dn%d" % gidx], 128 * i, 128, 128 * i) for i in range(32)]
        gemm(R, gsz, [tiles[i:i + 4] for i in range(0, 32, 4)], epi_dn, xT=aT)
        c0g += gsz
    sumsq_finish(R, D, R.pss[0])


def build_C(nc, P, layer):
    hT = P.dram("hT", [D, TOK], F32, kind="ExternalInput")
    cst_d = P.dram("cst", [128, C_NBASE + 64], F32, kind="ExternalInput")
    W = Weights(P, c_wspecs(layer), c_wsrcs(layer))
    mixT = P.dram("mixT", [D, TOK], F32)
    hmid = P.dram("hmid", [D, TOK], F32)
    ffT = P.dram("ffT", [D, TOK], F32)
    R = Res(P, C_NBASE + 64)
    aT = P.sb("aT", [128, max(FFN_G), TOK], BF16)
    U = [P.sb("U%d" % i, [128, TOK + 2]) for i in range(2)]
    Y = [P.sb("Y%d" % i, [128, TOK]) for i in range(2)]
    Gt = P.sb("Gt", [128, TOK])
    for u in U:
        P.op("dve", lambda e, u=u: e.memset(u.t[:, 0:2], 0.0), writes=[u])
    P.dma("sp", R.cst.t[:], cst_d.t[:, :], writes=[R.cst], sem=R.cst)

    if layer == 0:
        oaT = P.dram("oaT", [2048, TOK], F32, kind="ExternalInput")
        obT = P.dram("obT", [2048, TOK], BF16, kind="ExternalInput")
        gaT = P.dram("gaT", [2048, TOK], F32, kind="ExternalInput")
        h1T = P.dram("h1T", [D, TOK], F32, kind="ExternalOutput")
        pF1 = P.dram("pF1", [9264, TOK], F32, kind="ExternalOutput")
        for h in range(16):
            a = R.next_stg()
            g = R.next_stg()
            rows = slice(128 * h, 128 * h + 128)
            P.dma("sp", a.t[:, 0:TOK], oaT.t[rows, :], writes=[a], sem=a)
            P.dma("sp", g.t[:, 0:TOK], gaT.t[rows, :], writes=[g], sem=g)
            sumsq_begin(R)
            sumsq_add(R, a.t[:, 0:TOK], [a])
            sumsq_finish(R, 128, R.pss[h % 2])
            P.op("act", lambda e, g=g: e.activation(out=g.t[:, 0:TOK], in_=g.t[:, 0:TOK], func=AF.Silu), reads=[g], writes=[g])
            P.op("dve", lambda e, a=a, h=h: e.scalar_tensor_tensor(out=a.t[:, 0:TOK], in0=a.t[:, 0:TOK], scalar=R.cst.t[:, C_X + h:C_X + h + 1],
                                                                  in1=R.rstd.t[:], op0=ALU.mult, op1=ALU.mult), reads=[a, R.rstd, R.cst], writes=[a])
            P.op("dve", lambda e, a=a, g=g, h=h: e.tensor_tensor(out=R.xT.t[:, h, :], in0=a.t[:, 0:TOK], in1=g.t[:, 0:TOK], op=ALU.mult),
                 reads=[a, g], writes=[R.xT])
        for h in range(16):
            P.dma("sp", R.xT.t[:, 16 + h, :], obT.t[128 * h:128 * h + 128, :], writes=[R.xT], sem=R.xT)
    else:
        plT = P.dram("plT", [1024, TOK], F32, kind="ExternalInput")
        ysT = P.dram("ysT", [3072, TOK], F32, kind="ExternalInput")
        zT = P.dram("zT", [3072, TOK], F32, kind="ExternalInput")
        h2T = P.dram("h2T", [D, TOK], F32, kind="ExternalOutput")
        yzT = P.dram("yzT", [3072, TOK], F32)
        for kt in range(8):
            a = R.next_stg()
            P.dma("sp", a.t[:, 0:TOK], plT.t[128 * kt:128 * kt + 128, :], writes=[a], sem=a)
            P.op("dve", lambda e, a=a, kt=kt: e.tensor_copy(out=aT.t[:, kt, :], in_=a.t[:, 0:TOK]), reads=[a], writes=[aT])
        for g in range(4):
            def epi_pool(c0, wd, pset, g=g):
                j = 2 * g + c0 // 128
                for c, (t0, tl) in enumerate(CH):
                    P.op("act", lambda e, c=c, t0=t0, tl=tl: e.activation(out=R.xT.t[:, j, t0:t0 + tl], in_=pset[c].t[:, :tl], func=AF.Identity,
                                                                         scale=R.cst.t[:, C_X + j:C_X + j + 1]), reads=[pset[c], R.cst], writes=[R.xT])
            pwg = T(W["poolw"].t[256 * g:256 * g + 256, :], "pool_w%d" % g)
            pwg.b = W["poolw"].b
            gemm(R, 2, [[(pwg, 0, 128, 0), (pwg, 128, 128, 128)]], epi_pool, xT=aT, kt0=2 * g)
        sumsq_begin(R)
        for kt in range(24):
            a = R.next_stg()
            b = R.next_stg()
            rows = slice(128 * kt, 128 * kt + 128)
            P.dma("sp", a.t[:, 0:TOK], ysT.t[rows, :], writes=[a], sem=a)
            P.dma("sp", b.t[:, 0:TOK], zT.t[rows, :], writes=[b], sem=b)
            P.op("act", lambda e, b=b: e.activation(out=b.t[:, 0:TOK], in_=b.t[:, 0:TOK], func=AF.Silu), reads=[b], writes=[b])
            P.op("dve", lambda e, a=a, b=b: e.tensor_tensor(out=a.t[:, 0:TOK], in0=a.t[:, 0:TOK], in1=b.t[:, 0:TOK], op=ALU.mult), reads=[a, b], writes=[a])
            P.dma("act", yzT.t[rows, :], a.t[:, 0:TOK], reads=[a], writes=[yzT.sub(kt)], sem=a)
            sumsq_add(R, a.t[:, 0:TOK], [a])
        sumsq_finish(R, 3072, R.pss[0])
        apply_norm(R, yzT, 24, C_X + 8, kt_off=8)

    sumsq_begin(R)

    def epi_out(c0, wd, pset):
        s = R.next_stg()
        evac(R, pset, wd, lambda t0, tl: s.t[:wd, t0:t0 + tl], s)
        P.dma("act", mixT.t[c0:c0 + wd, :], s.t[:wd, 0:TOK], reads=[s], writes=[mixT.sub(c0 // 128)], sem=s)
        sumsq_add(R, s.t[:, 0:TOK], [s])

    tiles = [(W["out"], 128 * i, 128, 128 * i) for i in range(32)]
    gemm(R, 32, [tiles[i:i + 3] for i in range(0, 32, 3)], epi_out)
    sumsq_finish(R, D, R.pss[0])
    norm_residual(R, mixT, hT, hmid, C_G1)
    apply_norm(R, hmid, 32, C_G2)
    ffn(R, P, W, ffT, aT, U, Y, Gt)
    if layer == 1:
        norm_residual(R, ffT, hmid, h2T, C_G3)
    if layer == 0:
        norm_residual(R, ffT, hmid, h1T, C_G3)
        apply_norm(R, h1T, 32, C_GN)
        st = epi_store(R, pF1, 0)
        tl2 = [(128 * i, 128) for i in range(72)] + [(9216, 48)]
        tl2 = [(W["nin%d" % min(c0 // 2304, 3)], c0 - 2304 * min(c0 // 2304, 3), wd, c0) for (c0, wd) in tl2]
        gemm(R, 32, [tl2[i:i + 3] for i in range(0, 73, 3)], lambda c0, wd, pset: st(c0, wd, pset, c0))


def c_wspecs(layer):
    sp = ([("poolw", "pool_w", 0, 1024, 0, 256)] if layer == 1 else []) + [("out", "w_out", 0, D, 0, D)]
    c0 = 0
    for g, gsz in enumerate(FFN_G):
        sp += [("upg%d" % g, "w_up", 0, D, 128 * c0, 128 * gsz), ("upv%d" % g, "w_up", 0, D, DFF + 128 * c0, 128 * gsz),
               ("dn%d" % g, "w_down", 128 * c0, 128 * gsz, 0, D)]
        c0 += gsz
    if layer == 0:
        sp += [("nin%d" % i, "w_in2", 0, D, 2304 * i, 2304) for i in range(3)] + [("nin3", "w_in2", 0, D, 6912, 9264 - 6912)]
    return sp


def c_wsrcs(layer):
    d = {"w_out": (D, D), "w_up": (D, 2 * DFF), "w_down": (DFF, D)}
    if layer == 0:
        d["w_in2"] = (D, 9264)
    else:
        d["pool_w"] = (1024, 256)
    return d


def host_C_consts(inp, layer):
    ng = inp["norm_g"][layer]
    cw = inp["ffn_conv_w"][layer]
    cwt = np.ascontiguousarray(cw.reshape(3, NT_UP, 128).transpose(2, 1, 0)).reshape(128, NT_UP * 3)
    cb = col128(inp["ffn_conv_b"][layer], NT_UP)
    gn = col128(inp["norm_g"][layer + 1, 0], 32) if layer + 1 < 2 else np.zeros((128, 32), np.float32)
    parts = [col128(ng[1], 32), col128(ng[2], 32), col128(ng[3], 32), gn, cwt, cb]
    x = np.zeros((128, 64), np.float32)
    if layer == 0:
        x[:, 0:16] = inp["hgrn_norm_g"][0].T
    else:
        x[:, 0:8] = col128(inp["pool_scale"][0], 8)
        x[:, 8:32] = col128(inp["ssm_norm_g"][0], 24)
    parts.append(x)
    return np.concatenate(parts, axis=1).astype(np.float32)


def col128(v, n):
    return np.ascontiguousarray(v.reshape(n, 128).T)


def strip_cols(full, c):
    return np.ascontiguousarray(full[:, PADL + 1024 * c:PADL + 1024 * c + TOK])


def host_C0(inp, oF_A0, oa_B0, ob_B0):
    h = np.concatenate([inp["meta_tokens"], inp["x"][0]], axis=0)
    oa_full = np.concatenate([o.reshape(256, LP) for o in oa_B0], axis=0)
    ob_full = np.concatenate([o.reshape(256, LP) for o in ob_B0], axis=0)
    cst = host_C_consts(inp, 0)
    maps = []
    for c in range(NCORE):
        maps.append({"hT": np.ascontiguousarray(h[1024 * c:1024 * c + TOK].T), "cst": cst, "w_out": inp["ab_w_out"][0],
                     "w_up": inp["ffn_w_up"][0], "w_down": inp["ffn_w_down"][0], "w_in2": inp["cd_w_in"][0],
                     "oaT": strip_cols(oa_full, c), "obT": strip_cols(ob_full, c), "gaT": np.ascontiguousarray(oF_A0[c][4096:6144])})
    return maps


def host_C1(inp, pF1_C0, h1T_C0, pooled_B1, y_B1):
    pl_full = np.concatenate(pooled_B1, axis=0)
    ys_full = np.concatenate([y.reshape(384, LP) for y in y_B1], axis=0)
    cst = host_C_consts(inp, 1)
    pool_w2 = np.ascontiguousarray(inp["pool_w"][0].reshape(1024, 256))
    maps = []
    for c in range(NCORE):
        maps.append({"hT": h1T_C0[c], "cst": cst, "w_out": inp["cd_w_out"][0], "w_up": inp["ffn_w_up"][1], "w_down": inp["ffn_w_down"][1],
                     "pool_w": pool_w2,
                     "plT": strip_cols(pl_full, c), "ysT": strip_cols(ys_full, c), "zT": np.ascontiguousarray(pF1_C0[c][1024:4096]),
                     })
    return maps


BL1 = 1664
K_ID, K_TRI, K_RM, K_CW, K_CB, K_SELW, K_CORR, K_D, K_DTA, K_E = 0, 128, 256, 256 + BL1, 256 + BL1 + 20, 256 + BL1 + 25, \
    256 + BL1 + 29, 256 + BL1 + 45, 256 + BL1 + 45 + 384, 256 + BL1 + 45 + 384 + 2
K_N = K_E + 768


def bc3(ap2d, n):
    return ap2d.unsqueeze(2).broadcast_to([ap2d.shape[0], ap2d.shape[1], n])


def build_B1(nc, P, LPv=LP):
    NBLK = LPv // BL1
    CPB = BL1 // 128
    pu = P.dram("pu", [128, LPv], F32, kind="ExternalInput")
    xin = P.dram("xin", [5, 128, LPv], F32, kind="ExternalInput")
    dtr = P.dram("dtr", [6, LPv], F32, kind="ExternalInput")
    kf = P.dram("kf", [128, K_N], F32, kind="ExternalInput")
    kb = P.dram("kb", [128, 128], BF16, kind="ExternalInput")
    pooled = P.dram("pooled", [128, LPv], F32, kind="ExternalOutput")
    yT = P.dram("yT", [3, 128, LPv], F32, kind="ExternalOutput")

    ps = [P.ps("ps%d" % i, [128, 512]) for i in range(8)]
    K = P.sb("kf_sb", [128, K_N])
    KB = P.sb("kb_sb", [128, 128], BF16)
    P.dma("sp", K.t[:], kf.t[:, :], writes=[K], sem=K)
    P.dma("sp", KB.t[:], kb.t[:, :], writes=[KB], sem=KB)
    identf = K.t[:, K_ID:K_ID + 128]
    triu = K.t[:, K_TRI:K_TRI + 128]
    identb = KB.t[:, :]

    HL = 16
    ub = [P.sb("ub%d" % i, [128, HL + BL1]) for i in range(2)]
    s2 = P.sb("s2", [128, HL + BL1])
    s4 = P.sb("s4", [128, HL + BL1])
    s8 = P.sb("s8", [128, HL + BL1])
    s16 = P.sb("s16", [128, HL + BL1])
    pm = [P.sb("pm%d" % i, [128, BL1]) for i in range(2)]
    for t in (s2, s4, s8, s16):
        P.op("dve", lambda e, t=t: e.memset(t.t[:], 0.0), writes=[t])
    for blk in range(NBLK):
        u = ub[blk % 2]
        m = pm[blk % 2]
        if blk == 0:
            P.op("dve", lambda e, u=u: e.memset(u.t[:, 0:HL], 0.0), writes=[u])
            P.dma("sp", u.t[:, HL:], pu.t[:, 0:BL1], writes=[u], sem=u)
        else:
            P.dma("sp", u.t[:, :], pu.t[:, blk * BL1 - HL:(blk + 1) * BL1], writes=[u], sem=u)
        W = HL + BL1
        for (dst, src, k) in ((s2, u, 1), (s4, s2, 2), (s8, s4, 4), (s16, s8, 8)):
            P.op("dve", lambda e, dst=dst, src=src, k=k: e.tensor_tensor(out=dst.t[:, k:W], in0=src.t[:, k:W], in1=src.t[:, 0:W - k], op=ALU.add),
                 reads=[src], writes=[dst])
        P.op("dve", lambda e, m=m: e.tensor_scalar(out=m.t[:], in0=s2.t[:, HL:], scalar1=K.t[:, K_SELW:K_SELW + 1], scalar2=None, op0=ALU.mult),
             reads=[s2, K], writes=[m])
        for wi, sw in ((1, s4), (2, s8), (3, s16)):
            P.op("dve", lambda e, m=m, wi=wi, sw=sw: e.scalar_tensor_tensor(out=m.t[:], in0=sw.t[:, HL:], scalar=K.t[:, K_SELW + wi:K_SELW + wi + 1],
                                                                           in1=m.t[:], op0=ALU.mult, op1=ALU.add), reads=[sw, m, K], writes=[m])
        if blk == 0:
            P.op("dve", lambda e, m=m: e.tensor_tensor(out=m.t[:, PADL:PADL + 16], in0=m.t[:, PADL:PADL + 16], in1=K.t[:, K_CORR:K_CORR + 16], op=ALU.mult),
                 reads=[m, K], writes=[m])
        P.op("dve", lambda e, m=m, u=u: e.tensor_tensor(out=m.t[:], in0=m.t[:], in1=u.t[:, HL:], op=ALU.subtract), reads=[m, u], writes=[m])
        P.dma("sp", pooled.t[:, blk * BL1:(blk + 1) * BL1], m.t[:], reads=[m], writes=[pooled.sub(blk)], sem=m)

    HC = 3
    raw = [P.sb("raw%d" % i, [128, HC + BL1]) for i in range(5)]
    cv = [P.sb("cv%d" % i, [128, BL1]) for i in range(5)]
    BTb = P.sb("BTb", [128, BL1], BF16)
    CTb = P.sb("CTb", [128, BL1], BF16)
    dtt = P.sb("dtt", [6, BL1])
    lat = P.sb("lat", [6, BL1])
    acst = P.sb("acst", [6, BL1])
    avec = P.sb("avec", [6, 1])
    yTb = [P.sb("yTb%d" % i, [128, BL1]) for i in range(3)]
    H = P.sb("H", [128, 384])
    Hb = P.sb("Hb", [128, 384], BF16)
    P.op("dve", lambda e: e.memset(H.t[:], 0.0), writes=[H])
    P.op("dve", lambda e: e.memset(Hb.t[:], 0.0), writes=[Hb])
    P.op("act", lambda e: e.activation(out=avec.t[:], in_=K.t[0:6, K_DTA + 1:K_DTA + 2], func=AF.Exp), reads=[K], writes=[avec])
    P.op("dve", lambda e: e.tensor_scalar(out=avec.t[:], in0=avec.t[:], scalar1=-1.0, scalar2=None, op0=ALU.mult), reads=[avec], writes=[avec])
    cbm = P.sb("cbm", [128, 128])
    xstok = P.sb("xstok", [128, 384])
    btok = P.sb("btok", [128, 128], BF16)
    dttok = P.sb("dttok", [128, 6])
    acstok = P.sb("acstok", [128, 6])
    diff = P.sb("diff", [128, 768])
    Gt = P.sb("Gt", [128, 768], BF16)
    xd = P.sb("xd", [128, 384])
    xdb = P.sb("xdb", [128, 384], BF16)
    xddb = P.sb("xddb", [128, 384], BF16)
    dec = P.sb("dec", [128, 6])
    eal = P.sb("eal", [128, 6])
    ea = P.sb("ea", [128, 6])
    t1 = P.sb("t1", [128, 384])
    t2 = P.sb("t2", [128, 384])
    onec = P.sb("onec", [128, 1])
    P.op("dve", lambda e: e.memset(onec.t[:], 1.0), writes=[onec])

    for blk in range(NBLK):
        b0 = blk * BL1
        for i in range(5):
            r = raw[i]
            if blk == 0:
                P.op("dve", lambda e, r=r: e.memset(r.t[:, 0:HC], 0.0), writes=[r])
                P.dma("sp", r.t[:, HC:], xin.t[i, :, 0:BL1], writes=[r], sem=r)
            else:
                P.dma("sp", r.t[:, :], xin.t[i, :, b0 - HC:b0 + BL1], writes=[r], sem=r)
            c = cv[i]
            cw = K_CW + 4 * i
            P.op("act", lambda e, r=r, c=c, cw=cw, i=i: e.activation(out=c.t[:], in_=r.t[:, 3:3 + BL1], func=AF.Identity,
                                                                    scale=K.t[:, cw + 3:cw + 4], bias=K.t[:, K_CB + i:K_CB + i + 1]),
                 reads=[r, K], writes=[c])
            for k in range(3):
                P.op("dve", lambda e, r=r, c=c, cw=cw, k=k: e.scalar_tensor_tensor(out=c.t[:], in0=r.t[:, k:k + BL1], scalar=K.t[:, cw + k:cw + k + 1],
                                                                                 in1=c.t[:], op0=ALU.mult, op1=ALU.add), reads=[r, c, K], writes=[c])
            P.op("act", lambda e, c=c: e.activation(out=c.t[:], in_=c.t[:], func=AF.Silu), reads=[c], writes=[c])
        P.op("dve", lambda e: e.tensor_copy(out=BTb.t[:], in_=cv[3].t[:]), reads=[cv[3]], writes=[BTb])
        P.op("dve", lambda e: e.tensor_copy(out=CTb.t[:], in_=cv[4].t[:]), reads=[cv[4]], writes=[CTb])
        P.dma("sp", dtt.t[:], dtr.t[:, b0:b0 + BL1], writes=[dtt], sem=dtt)
        P.op("act", lambda e: e.activation(out=dtt.t[:], in_=dtt.t[:], func=AF.Exp, bias=K.t[0:6, K_DTA:K_DTA + 1]), reads=[dtt, K], writes=[dtt])
        P.op("act", lambda e: e.activation(out=dtt.t[:], in_=dtt.t[:], func=AF.Ln, bias=onec.t[0:6, :]), reads=[dtt, onec], writes=[dtt])
        P.op("dve", lambda e: e.tensor_scalar(out=lat.t[:], in0=dtt.t[:], scalar1=avec.t[:, 0:1], scalar2=None, op0=ALU.mult), reads=[dtt, avec], writes=[lat])
        P.op("dve", lambda e: e.tensor_tensor_scan(out=acst.t[:], data0=K.t[0:6, K_RM:K_RM + BL1], data1=lat.t[:], initial=0.0, op0=ALU.mult, op1=ALU.add),
             reads=[lat, K], writes=[acst])
        for c in range(CPB):
            cs = slice(c * 128, (c + 1) * 128)
            pA, pB, pR0, pR1, pO, pD, pH, pC = ps
            for j in range(3):
                P.op("pe", lambda e, j=j, cs=cs: e.matmul(pA.t[:, j * 128:(j + 1) * 128], lhsT=cv[j].t[:, cs], rhs=identf, start=True, stop=True),
                     reads=[cv[j], K], writes=[pA])
            P.op("pe", lambda e, cs=cs: e.matmul(pA.t[:, 384:512], lhsT=BTb.t[:, cs], rhs=identb, start=True, stop=True), reads=[BTb, KB], writes=[pA])
            P.op("pe", lambda e, cs=cs: e.matmul(pB.t[:, 0:6], lhsT=dtt.t[:, cs], rhs=K.t[0:6, K_ID:K_ID + 6], start=True, stop=True), reads=[dtt, K], writes=[pB])
            P.op("pe", lambda e, cs=cs: e.matmul(pB.t[:, 8:14], lhsT=acst.t[:, cs], rhs=K.t[0:6, K_ID:K_ID + 6], start=True, stop=True), reads=[acst, K], writes=[pB])
            P.op("act", lambda e: e.activation(out=xstok.t[:], in_=pA.t[:, 0:384], func=AF.Copy), reads=[pA], writes=[xstok])
            P.op("dve", lambda e: e.tensor_copy(out=btok.t[:], in_=pA.t[:, 384:512]), reads=[pA], writes=[btok])
            P.op("dve", lambda e: e.tensor_copy(out=dttok.t[:], in_=pB.t[:, 0:6]), reads=[pB], writes=[dttok])
            P.op("dve", lambda e: e.tensor_copy(out=acstok.t[:], in_=pB.t[:, 8:14]), reads=[pB], writes=[acstok])
            for h in range(6):
                pr = pR0 if h < 4 else pR1
                hh = h % 4
                P.op("pe", lambda e, h=h, pr=pr, hh=hh, cs=cs: e.matmul(pr.t[:, hh * 128:(hh + 1) * 128], lhsT=K.t[0:6, K_E + h * 128:K_E + (h + 1) * 128],
                                                                       rhs=acst.t[:, cs], start=True, stop=True), reads=[K, acst], writes=[pr])
            P.op("pe", lambda e, cs=cs: e.matmul(pC.t[:, 0:128], lhsT=BTb.t[:, cs], rhs=CTb.t[:, cs], start=True, stop=True), reads=[BTb, CTb], writes=[pC])
            P.op("dve", lambda e: e.tensor_tensor(out=cbm.t[:], in0=pC.t[:, 0:128], in1=triu, op=ALU.mult), reads=[pC, K], writes=[cbm])
            P.op("dve", lambda e: e.tensor_tensor(out=diff.t[:, 0:512].rearrange("p (h t) -> p h t", t=128), in0=pR0.t[:, :].rearrange("p (h t) -> p h t", t=128),
                                                  in1=bc3(acstok.t[:, 0:4], 128), op=ALU.subtract), reads=[pR0, acstok], writes=[diff])
            P.op("dve", lambda e: e.tensor_tensor(out=diff.t[:, 512:768].rearrange("p (h t) -> p h t", t=128), in0=pR1.t[:, 0:256].rearrange("p (h t) -> p h t", t=128),
                                                  in1=bc3(acstok.t[:, 4:6], 128), op=ALU.subtract), reads=[pR1, acstok], writes=[diff])
            P.op("dve", lambda e: e.tensor_copy(out=eal.t[:, 0:4], in_=pR0.t[:, :].rearrange("p (h t) -> p h t", t=128)[:, :, 127]), reads=[pR0], writes=[eal])
            P.op("dve", lambda e: e.tensor_copy(out=eal.t[:, 4:6], in_=pR1.t[:, 0:256].rearrange("p (h t) -> p h t", t=128)[:, :, 127]), reads=[pR1], writes=[eal])
            P.op("dve", lambda e: e.tensor_tensor(out=dec.t[:], in0=eal.t[:], in1=acstok.t[:], op=ALU.subtract), reads=[eal, acstok], writes=[dec])
            P.op("act", lambda e: e.activation(out=diff.t[:], in_=diff.t[:], func=AF.Exp), reads=[diff], writes=[diff])
            P.op("act", lambda e: e.activation(out=dec.t[:], in_=dec.t[:], func=AF.Exp), reads=[dec], writes=[dec])
            P.op("act", lambda e: e.activation(out=eal.t[:], in_=eal.t[:], func=AF.Exp), reads=[eal], writes=[eal])
            P.op("act", lambda e: e.activation(out=ea.t[:], in_=acstok.t[:], func=AF.Exp), reads=[acstok], writes=[ea])
            P.op("dve", lambda e: e.scalar_tensor_tensor(out=Gt.t[:].rearrange("p (h t) -> p h t", t=128), in0=diff.t[:].rearrange("p (h t) -> p h t", t=128),
                                                         scalar=1.0, in1=cbm.t[:].unsqueeze(1).broadcast_to([128, 6, 128]), op0=ALU.min, op1=ALU.mult),
                 reads=[diff, cbm], writes=[Gt])
            P.op("dve", lambda e: e.tensor_tensor(out=xd.t[:].rearrange("p (h q) -> p h q", q=64), in0=xstok.t[:].rearrange("p (h q) -> p h q", q=64),
                                                  in1=bc3(dttok.t[:], 64), op=ALU.mult), reads=[xstok, dttok], writes=[xd])
            P.op("act", lambda e: e.activation(out=xdb.t[:], in_=xd.t[:], func=AF.Copy), reads=[xd], writes=[xdb])
            P.op("dve", lambda e: e.tensor_tensor(out=xddb.t[:].rearrange("p (h q) -> p h q", q=64), in0=xd.t[:].rearrange("p (h q) -> p h q", q=64),
                                                  in1=bc3(dec.t[:], 64), op=ALU.mult), reads=[xd, dec], writes=[xddb])
            P.op("pe", lambda e, cs=cs: e.matmul(pO.t[:, 0:384], lhsT=CTb.t[:, cs], rhs=Hb.t[:], start=True, stop=True), reads=[CTb, Hb], writes=[pO])
            for h in range(6):
                P.op("pe", lambda e, h=h: e.matmul(pD.t[:, h * 64:(h + 1) * 64], lhsT=Gt.t[:, h * 128:(h + 1) * 128], rhs=xdb.t[:, h * 64:(h + 1) * 64],
                                                   start=True, stop=True), reads=[Gt, xdb], writes=[pD])
            P.op("pe", lambda e: e.matmul(pH.t[:, 0:384], lhsT=btok.t[:], rhs=xddb.t[:], start=True, stop=True), reads=[btok, xddb], writes=[pH])
            P.op("dve", lambda e: e.tensor_tensor(out=t1.t[:].rearrange("p (h q) -> p h q", q=64), in0=pO.t[:, 0:384].rearrange("p (h q) -> p h q", q=64),
                                                  in1=bc3(ea.t[:], 64), op=ALU.mult), reads=[pO, ea], writes=[t1])
            P.op("dve", lambda e: e.tensor_tensor(out=t1.t[:], in0=t1.t[:], in1=pD.t[:, 0:384], op=ALU.add), reads=[t1, pD], writes=[t1])
            P.op("dve", lambda e: e.tensor_tensor(out=t2.t[:], in0=xstok.t[:], in1=K.t[:, K_D:K_D + 384], op=ALU.mult), reads=[xstok, K], writes=[t2])
            P.op("dve", lambda e: e.tensor_tensor(out=t1.t[:], in0=t1.t[:], in1=t2.t[:], op=ALU.add), reads=[t1, t2], writes=[t1])
            P.op("dve", lambda e: e.tensor_tensor(out=H.t[:].rearrange("p (h q) -> p h q", q=64), in0=H.t[:].rearrange("p (h q) -> p h q", q=64),
                                                  in1=bc3(eal.t[:], 64), op=ALU.mult), reads=[H, eal], writes=[H])
            P.op("dve", lambda e: e.tensor_tensor(out=H.t[:], in0=H.t[:], in1=pH.t[:, 0:384], op=ALU.add), reads=[H, pH], writes=[H])
            P.op("act", lambda e: e.activation(out=Hb.t[:], in_=H.t[:], func=AF.Copy), reads=[H], writes=[Hb])
            for j in range(3):
                P.op("pe", lambda e, j=j: e.matmul(pB.t[:, 128 + j * 128:128 + (j + 1) * 128] if False else pC.t[:, 128 + j * 128:256 + j * 128],
                                                   lhsT=t1.t[:, j * 128:(j + 1) * 128], rhs=identf, start=True, stop=True), reads=[t1, K], writes=[pC])
            for j in range(3):
                P.op("act", lambda e, j=j, cs=cs: e.activation(out=yTb[j].t[:, cs], in_=pC.t[:, 128 + j * 128:256 + j * 128], func=AF.Copy),
                     reads=[pC], writes=[yTb[j]])
        for j in range(3):
            P.dma("sp", yT.t[j, :, b0:b0 + BL1], yTb[j].t[:], reads=[yTb[j]], writes=[yT.sub((j, blk))], sem=yTb[j])


POOLW = (2, 4, 8, 16)


def consts_B1(inp, c):
    kf = np.zeros((128, K_N), np.float32)
    kf[:, K_ID:K_ID + 128] = np.eye(128)
    s = np.arange(128)[:, None]
    t = np.arange(128)[None, :]
    kf[:, K_TRI:K_TRI + 128] = (s <= t)
    rm = np.ones(BL1, np.float32)
    rm[0::128] = 0
    kf[:, K_RM:K_RM + BL1] = rm[None, :]
    cw = inp["ssm_conv_w"][0]
    cb = inp["ssm_conv_b"][0]
    chans = [np.arange(384 * c + 128 * j, 384 * c + 128 * j + 128) for j in range(3)] + \
            [3072 + np.arange(128 * c, 128 * c + 128), 4096 + np.arange(128 * c, 128 * c + 128)]
    for i, ch in enumerate(chans):
        kf[:, K_CW + 4 * i:K_CW + 4 * i + 4] = cw[:, ch].T
        kf[:, K_CB + i] = cb[ch]
    g = c // 2
    w = POOLW[g]
    kf[:, K_SELW + g] = 1.0 / w
    tt = np.arange(16)
    kf[:, K_CORR:K_CORR + 16] = (w / np.minimum(tt + 1, w))[None, :]
    kf[:, K_D:K_D + 384] = np.repeat(inp["ssm_d"][0][6 * c:6 * c + 6], 64)[None, :]
    kf[0:6, K_DTA] = inp["ssm_dt_bias"][0][6 * c:6 * c + 6]
    kf[0:6, K_DTA + 1] = inp["ssm_a_log"][0][6 * c:6 * c + 6]
    for h in range(6):
        kf[h, K_E + 128 * h:K_E + 128 * (h + 1)] = 1.0
    return kf, np.eye(128, dtype=np.float32).astype(NPBF)


def assemble(outs):
    rows = outs[0].shape[0]
    full = np.zeros((rows, LP), outs[0].dtype)
    full[:, PADL:PADL + TOK] = outs[0]
    for c in range(1, NCORE):
        full[:, PADL + 1024 * c + 16:PADL + 1024 * c + TOK] = outs[c][:, 16:]
    return full


def host_B1(inp, pF1, LPv=LP):
    full = assemble(pF1)[:, :LPv]
    maps = []
    for c in range(NCORE):
        kf, kb = consts_B1(inp, c)
        xin = np.stack([full[4096 + 384 * c + 128 * j:4096 + 384 * c + 128 * j + 128] for j in range(3)] +
                       [full[7168 + 128 * c:7168 + 128 * c + 128], full[8192 + 128 * c:8192 + 128 * c + 128]])
        maps.append({"pu": np.ascontiguousarray(full[128 * c:128 * c + 128]), "xin": np.ascontiguousarray(xin),
                     "dtr": np.ascontiguousarray(full[9216 + 6 * c:9216 + 6 * c + 6]), "kf": kf, "kb": kb})
    return maps


def _run(build_fn, maps):
    nc = bass.Bass("TRN2", target_bir_lowering=False)
    with contextlib.ExitStack() as stack:
        P = Prog(nc, stack)
        build_fn(nc, P)
        P.wait_all_dma("sp")
        P.finalize()
    res = run_bass_kernel_spmd(nc, maps, core_ids=list(range(NCORE)))
    return res.results


def kernel(**inputs):
    inp = {k: np.asarray(v) for k, v in inputs.items()}
    rA = _run(build_A0, host_A0(inp))
    oF = [r["oF"] for r in rA]
    oB = [np.asarray(r["oB"]).astype(NPBF) for r in rA]
    rB = _run(lambda nc, P: build_B0(nc, P, LP), host_B0(inp, oF, oB, LP))
    oa = [r["oa"] for r in rB]
    ob = [np.asarray(r["ob"]).astype(NPBF) for r in rB]
    del rA, rB
    rC = _run(lambda nc, P: build_C(nc, P, 0), host_C0(inp, oF, oa, ob))
    h1T = [r["h1T"] for r in rC]
    pF1 = [r["pF1"] for r in rC]
    rB1 = _run(lambda nc, P: build_B1(nc, P, LP), host_B1(inp, pF1, LP))
    pooled = [r["pooled"] for r in rB1]
    ys = [r["yT"] for r in rB1]
    rC1 = _run(lambda nc, P: build_C(nc, P, 1), host_C1(inp, pF1, h1T, pooled, ys))
    out = np.concatenate([np.asarray(r["h2T"])[:, 16:TOK].T for r in rC1], axis=0)
    return np.ascontiguousarray(out.reshape(1, SEQ, D).astype(np.float32))
```

```python
import contextlib
import numpy as np
import ml_dtypes
import concourse.bass as bass
import concourse.mybir as mybir
from concourse.bass_utils import run_bass_kernel_spmd

F32 = mybir.dt.float32
BF16 = mybir.dt.bfloat16
AF = mybir.ActivationFunctionType
ALU = mybir.AluOpType
NPBF = ml_dtypes.bfloat16

D = 4096
TOK = 1040
CH = [(0, 347), (347, 347), (694, 346)]
NCORE = 8
SEQ = 8192
NMETA = 16
LTOT = 8208
PADL = 112
LP = 8320
DFF = 11008
EPS = 1e-6

COMPUTE = ("pe", "act", "dve", "pool")


class Buf:
    __slots__ = ("name", "w", "r", "dcount")

    def __init__(self, name):
        self.name = name
        self.w = None
        self.r = {}
        self.dcount = 0


class T:
    def __init__(self, ten, name):
        self.t = ten
        self.b = Buf(name)
        self.subs = {}

    def __getitem__(self, k):
        return self.t[k]

    def sub(self, key):
        s = self.subs.get(key)
        if s is None:
            s = T(self.t, "%s.%s" % (self.b.name, key))
            self.subs[key] = s
        return s


class _Recorder:
    def __getattr__(self, name):
        def cap(*a, **k):
            self.call = (name, a, k)
        return cap

    def replay(self, e):
        name, a, k = self.call
        return getattr(e, name)(*a, **k)


class Prog:
    def __init__(self, nc, stack):
        self.nc = nc
        self.stack = stack
        self.q = {e: [] for e in ("pe", "act", "dve", "pool", "sp")}
        self.waited = {e: {} for e in self.q}
        self.dma_sems = {}
        self.dma_final = {}

    def sb(self, name, shape, dt=F32):
        return T(self.stack.enter_context(self.nc.sbuf_tensor(name, list(shape), dt)), name)

    def ps(self, name, shape, dt=F32):
        return T(self.stack.enter_context(self.nc.psum_tensor(name, list(shape), dt)), name)

    def dram(self, name, shape, dt=F32, kind="Internal"):
        return T(self.nc.dram_tensor(name, list(shape), dt, kind=kind).ap(), name)

    def _collect(self, eng, reads, writes):
        need = {}

        def add(ev, is_writer):
            if ev is None:
                return
            kind, k, v = ev
            if kind == "c" and k == eng:
                if eng == "pe" or not is_writer:
                    return
            key = (kind, k)
            if need.get(key, -1) < v:
                need[key] = v

        for b in reads:
            add(b.b.w, True)
        for b in writes:
            add(b.b.w, True)
            for ev in b.b.r.values():
                add(ev, False)
        waits = []
        wd = self.waited[eng]
        for key, v in need.items():
            if wd.get(key, -1) >= v:
                continue
            wd[key] = v
            waits.append((key, v))
        return waits

    def _commit(self, ev, reads, writes):
        for b in writes:
            b.b.w = ev
            b.b.r = {}
        for b in reads:
            b.b.r[(ev[0], ev[1])] = ev

    def op(self, eng, fn, reads=(), writes=()):
        waits = self._collect(eng, reads, writes)
        idx = len(self.q[eng])
        rec = _Recorder()
        fn(rec)
        self.q[eng].append(dict(fn=rec.replay, waits=waits, sig=False, dma=None))
        self._commit(("c", eng, idx), reads, writes)

    def dma(self, queue, out, in_, reads=(), writes=(), sem=None, **kw):
        waits = self._collect(queue, reads, writes)
        sbn = sem.b.name.split(".")[0]
        cnt = self.dma_final.get(sbn, 0) + 16
        self.dma_final[sbn] = cnt
        self.dma_sems.setdefault(sbn, None)
        self.q[queue].append(dict(fn=lambda e: e.dma_start(out=out, in_=in_, **kw), waits=waits,
                                  sig=False, dma=sbn, inc=16))
        self._commit(("d", sbn, cnt), reads, writes)

    def allgather(self, in_ap, out_ap, reads=(), writes=()):
        if not hasattr(self, "ccchain"):
            self.ccchain = T(None, "ccchain")
        reads = list(reads) + [self.ccchain]
        writes = list(writes) + [self.ccchain]
        waits = self._collect("pool", reads, writes)
        cnt = self.dma_final.get("ccsem", 0) + 1
        self.dma_final["ccsem"] = cnt
        self.dma_sems.setdefault("ccsem", None)
        groups = [list(range(NCORE))]
        self.q["pool"].append(dict(fn=lambda e: e.collective_compute("AllGather", ALU.bypass, replica_groups=groups,
                                                                     ins=[in_ap], outs=[out_ap]),
                                   waits=waits, sig=False, dma="ccsem", inc=1))
        self._commit(("d", "ccsem", cnt), reads, writes)

    def wait_all_dma(self, eng):
        waits = [(("d", n), c) for n, c in self.dma_final.items()]
        self.q[eng].append(dict(fn=None, waits=waits, sig=False, dma=None))

    def finalize(self):
        nc = self.nc
        for eng, lst in self.q.items():
            for rec in lst:
                for (kind, k), v in rec["waits"]:
                    if kind == "c":
                        self.q[k][v]["sig"] = True
        for eng in COMPUTE:
            c = 0
            for rec in self.q[eng]:
                if rec["sig"]:
                    c += 1
                rec["cnt"] = c
        sems = {}
        for eng in COMPUTE:
            sems[eng] = self.stack.enter_context(nc.semaphore("c_" + eng))
        for name in self.dma_sems:
            self.dma_sems[name] = self.stack.enter_context(nc.semaphore("d_" + name))
        block = self.stack.enter_context(nc.Block())

        def emit(eng):
            def body(e):
                for rec in self.q[eng]:
                    for (kind, k), v in rec["waits"]:
                        if kind == "c":
                            e.wait_ge(sems[k], self.q[k][v]["cnt"])
                        else:
                            e.wait_ge(self.dma_sems[k], v)
                    if rec["fn"] is None:
                        continue
                    ins = rec["fn"](e)
                    if rec["dma"] is not None:
                        ins.then_inc(self.dma_sems[rec["dma"]], rec["inc"])
                    elif rec["sig"]:
                        ins.then_inc(sems[eng], 1)
            return body

        block.tensor(emit("pe"))
        block.scalar(emit("act"))
        block.vector(emit("dve"))
        block.gpsimd(emit("pool"))
        block.sync(emit("sp"))
        self.n_instr = {e: len(l) for e, l in self.q.items()}


SLOT_ELEMS = 12288


class Res:
    def __init__(self, P, nconst):
        self.P = P
        self.ps = [P.ps("ps%d" % i, [128, 512]) for i in range(8)]
        self.pss = [self.ps[0:3], self.ps[3:6]]
        self.slots = [P.sb("wslot%d" % i, [128, SLOT_ELEMS], BF16) for i in range(2)]
        self.slot_i = 0
        self.xT = P.sb("xT", [128, 32, TOK], BF16)
        self.stg = [P.sb("stg%d" % i, [128, TOK + 4], F32) for i in range(4)]
        self.stg_i = 0
        self.stb = [P.sb("stb%d" % i, [128, TOK], BF16) for i in range(2)]
        self.stb_i = 0
        self.acc = P.sb("acc", [128, TOK], F32)
        self.rstd = P.sb("rstd", [128, TOK], F32)
        self.sq = P.sb("sq", [128, TOK], F32)
        self.ones = P.sb("ones", [128, 128], F32)
        self.cst = P.sb("cst_sb", [128, nconst], F32)
        P.op("dve", lambda e: e.memset(self.ones.t[:], 1.0), writes=[self.ones])
        self.epsc = P.sb("epsc", [128, 1], F32)
        P.op("dve", lambda e: e.memset(self.epsc.t[:], EPS), writes=[self.epsc])

    def next_stg(self):
        s = self.stg[self.stg_i % len(self.stg)]
        self.stg_i += 1
        return s

    def next_stb(self):
        s = self.stb[self.stb_i % len(self.stb)]
        self.stb_i += 1
        return s

    def next_slot(self):
        s = self.slots[self.slot_i % len(self.slots)]
        self.slot_i += 1
        return s


def sumsq_begin(R):
    R.first_sq = True


def sumsq_add(R, src_ap, reads):
    P = R.P
    if R.first_sq:
        P.op("act", lambda e: e.activation(out=R.acc.t[:], in_=src_ap, func=AF.Square), reads=reads, writes=[R.acc])
        R.first_sq = False
    else:
        P.op("act", lambda e: e.activation(out=R.sq.t[:], in_=src_ap, func=AF.Square), reads=reads, writes=[R.sq])
        P.op("dve", lambda e: e.tensor_tensor(out=R.acc.t[:], in0=R.acc.t[:], in1=R.sq.t[:], op=ALU.add),
             reads=[R.sq, R.acc], writes=[R.acc])


def sumsq_finish(R, nfeat, pset):
    P = R.P
    for c, (t0, tl) in enumerate(CH):
        P.op("pe", lambda e, c=c, t0=t0, tl=tl: e.matmul(pset[c].t[:, :tl], lhsT=R.ones.t[:], rhs=R.acc.t[:, t0:t0 + tl],
                                                         start=True, stop=True),
             reads=[R.ones, R.acc], writes=[pset[c]])
        P.op("act", lambda e, c=c, t0=t0, tl=tl: e.activation(out=R.rstd.t[:, t0:t0 + tl], in_=pset[c].t[:, :tl], func=AF.Sqrt,
                                                              scale=1.0 / nfeat, bias=R.epsc.t[:, 0:1]),
             reads=[pset[c], R.epsc], writes=[R.rstd])
    P.op("dve", lambda e: e.reciprocal(out=R.rstd.t[:], in_=R.rstd.t[:]), reads=[R.rstd], writes=[R.rstd])


def apply_norm(R, src, nkt, gcol, kt_off=0):
    P = R.P
    for kt in range(nkt):
        s = R.next_stg()
        P.dma("sp", s.t[:, 0:TOK], src.t[kt * 128:(kt + 1) * 128, :], reads=[src.sub(kt)], writes=[s], sem=s)
        P.op("dve", lambda e, s=s, kt=kt: e.scalar_tensor_tensor(
            out=R.xT.t[:, kt_off + kt, :], in0=s.t[:, 0:TOK], scalar=R.cst.t[:, gcol + kt:gcol + kt + 1],
            in1=R.rstd.t[:], op0=ALU.mult, op1=ALU.mult), reads=[s, R.rstd, R.cst], writes=[R.xT])


def norm_to_xT(R, src, nkt, gcol, kt_off=0):
    P = R.P
    sumsq_begin(R)
    for kt in range(nkt):
        s = R.next_stg()
        P.dma("sp", s.t[:, 0:TOK], src.t[kt * 128:(kt + 1) * 128, :], reads=[src.sub(kt)], writes=[s], sem=s)
        sumsq_add(R, s.t[:, 0:TOK], [s])
    sumsq_finish(R, nkt * 128, R.pss[0])
    apply_norm(R, src, nkt, gcol, kt_off)


def norm_residual(R, y, h, hout, gcol, nkt=32):
    P = R.P
    sumsq_begin(R)
    for kt in range(nkt):
        a = R.next_stg()
        b = R.next_stg()
        rows = slice(kt * 128, (kt + 1) * 128)
        P.dma("sp", a.t[:, 0:TOK], y.t[rows, :], reads=[y.sub(kt)], writes=[a], sem=a)
        P.dma("sp", b.t[:, 0:TOK], h.t[rows, :], reads=[h.sub(kt)], writes=[b], sem=b)
        P.op("dve", lambda e, a=a, kt=kt: e.scalar_tensor_tensor(out=a.t[:, 0:TOK], in0=a.t[:, 0:TOK], scalar=R.cst.t[:, gcol + kt:gcol + kt + 1],
                                                                in1=R.rstd.t[:], op0=ALU.mult, op1=ALU.mult), reads=[a, R.rstd, R.cst], writes=[a])
        P.op("dve", lambda e, a=a, b=b: e.tensor_tensor(out=b.t[:, 0:TOK], in0=a.t[:, 0:TOK], in1=b.t[:, 0:TOK], op=ALU.add),
             reads=[a, b], writes=[b])
        P.dma("act", hout.t[rows, :], b.t[:, 0:TOK], reads=[b], writes=[hout.sub(kt)], sem=b)
        sumsq_add(R, b.t[:, 0:TOK], [b])
    sumsq_finish(R, nkt * 128, R.pss[0])


def gemm(R, KT, panels, epi, xT=None, kt0=0):
    P = R.P
    xT = xT or R.xT
    ti = 0
    for panel in panels:
        slot = R.next_slot()
        pw = sum(t[2] for t in panel)
        assert KT * pw <= SLOT_ELEMS, (KT, pw)
        view = slot.t[:, 0:KT * pw].rearrange("p (k n) -> p k n", n=pw)
        runs = []
        off = 0
        for (w, c0, wd, key) in panel:
            if runs and runs[-1][0] is w and runs[-1][1] + runs[-1][2] == c0:
                runs[-1][2] += wd
            else:
                runs.append([w, c0, wd, off])
            off += wd
        for (w, c0, wd, o) in runs:
            kstep = 8
            for k0 in range(0, KT, kstep):
                k1 = min(KT, k0 + kstep)
                src = w.t[k0 * 128:k1 * 128, c0:c0 + wd].rearrange("(k p) n -> p k n", p=128)
                P.dma("pool", view[:, k0:k1, o:o + wd], src, reads=[w], writes=[slot], sem=slot)
        off = 0
        for (w, c0, wd, key) in panel:
            pset = R.pss[ti % 2]
            ti += 1
            for kt in range(KT):
                for c, (t0, tl) in enumerate(CH):
                    P.op("pe", lambda e, c=c, t0=t0, tl=tl, kt=kt, off=off, wd=wd, pset=pset, view=view: e.matmul(
                        pset[c].t[:wd, :tl], lhsT=view[:, kt, off:off + wd], rhs=xT.t[:, kt0 + kt, t0:t0 + tl],
                        start=(kt == 0), stop=(kt == KT - 1)), reads=[slot, xT], writes=[pset[c]])
            epi(key, wd, pset)
            off += wd


class Weights:
    def __init__(self, P, specs, srcs):
        self.src = {n: P.dram(n, list(shp), F32, kind="ExternalInput") for n, shp in srcs.items()}
        self.pieces = {}
        for (key, src, r0, nr, c0, ncl) in specs:
            self.pieces[key] = T(self.src[src].t[r0:r0 + nr, c0:c0 + ncl], "%s_%s" % (src, key))

    def __getitem__(self, key):
        return self.pieces[key]


def pack_shards(specs, arrays):
    out = []
    for r in range(NCORE):
        parts = []
        for (key, nr, ncl) in specs:
            a = arrays[key]
            assert a.shape == (nr, ncl), (key, a.shape, nr, ncl)
            parts.append(np.ascontiguousarray(a[r * (nr // 8):(r + 1) * (nr // 8)]).reshape(-1))
        out.append(np.concatenate(parts).astype(np.float32, copy=False))
    return out


def evac(R, pset, wd, dst_ap_fn, dst, eng="act"):
    P = R.P
    for c, (t0, tl) in enumerate(CH):
        if eng == "act":
            P.op("act", lambda e, c=c, t0=t0, tl=tl: e.activation(out=dst_ap_fn(t0, tl), in_=pset[c].t[:wd, :tl], func=AF.Copy),
                 reads=[pset[c]], writes=[dst])
        else:
            P.op("dve", lambda e, c=c, t0=t0, tl=tl: e.tensor_copy(out=dst_ap_fn(t0, tl), in_=pset[c].t[:wd, :tl]),
                 reads=[pset[c]], writes=[dst])


def epi_store(R, dst, row0, bf=False, eng="act"):
    P = R.P

    def f(c0, wd, pset, r0):
        s = R.next_stb() if bf else R.next_stg()
        evac(R, pset, wd, lambda t0, tl: s.t[:wd, t0:t0 + tl], s, eng=eng)
        P.dma("act", dst.t[r0:r0 + wd, :], s.t[:wd, 0:TOK], reads=[s], writes=[dst.sub(r0 // 128)], sem=s)
    return f


def run_prog(build_fn, in_maps, trace=False):
    nc = bass.Bass("TRN2", target_bir_lowering=False)
    with contextlib.ExitStack() as stack:
        P = Prog(nc, stack)
        build_fn(nc, P)
        P.wait_all_dma("sp")
        P.finalize()
    res = run_bass_kernel_spmd(nc, in_maps, core_ids=list(range(NCORE)), trace=trace)
    return res, P


A0_OF_ROWS = 6144
A0_OB_ROWS = 9280
A0_NCONST = 32 + 8 + 4
A0_WSPECS = [("in%d" % i, "w_in", 0, D, 2304 * i, 2304) for i in range(4)] + [("in4", "w_in", 0, D, 9216, 576),
                                                                             ("uq", "w_uq", 0, 1024, 0, 3072), ("ukv", "w_ukv", 0, 512, 0, 4096)]
A0_WSRCS = {"w_in": (D, 9792), "w_uq": (1024, 3072), "w_ukv": (512, 4096)}


def rope_epi(R, P, ps1, ps2, wd, cosT, sinT, dst, r1, r2):
    t1 = R.next_stg()
    t2 = R.next_stg()
    o1 = R.next_stb()
    o2 = R.next_stb()
    for c, (t0, tl) in enumerate(CH):
        sl = slice(t0, t0 + tl)
        P.op("dve", lambda e, c=c, sl=sl, tl=tl: e.tensor_tensor(out=t1.t[:wd, sl], in0=ps1[c].t[:wd, :tl], in1=cosT.t[:wd, sl], op=ALU.mult),
             reads=[ps1[c], cosT], writes=[t1])
        P.op("dve", lambda e, c=c, sl=sl, tl=tl: e.tensor_tensor(out=t2.t[:wd, sl], in0=ps2[c].t[:wd, :tl], in1=sinT.t[:wd, sl], op=ALU.mult),
             reads=[ps2[c], sinT], writes=[t2])
    P.op("dve", lambda e: e.tensor_tensor(out=o1.t[:wd, :], in0=t1.t[:wd, 0:TOK], in1=t2.t[:wd, 0:TOK], op=ALU.subtract),
         reads=[t1, t2], writes=[o1])
    P.dma("act", dst.t[r1:r1 + wd, :], o1.t[:wd, :], reads=[o1], writes=[dst.sub(("r", r1))], sem=o1)
    t3 = R.next_stg()
    t4 = R.next_stg()
    for c, (t0, tl) in enumerate(CH):
        sl = slice(t0, t0 + tl)
        P.op("dve", lambda e, c=c, sl=sl, tl=tl: e.tensor_tensor(out=t3.t[:wd, sl], in0=ps1[c].t[:wd, :tl], in1=sinT.t[:wd, sl], op=ALU.mult),
             reads=[ps1[c], sinT], writes=[t3])
        P.op("dve", lambda e, c=c, sl=sl, tl=tl: e.tensor_tensor(out=t4.t[:wd, sl], in0=ps2[c].t[:wd, :tl], in1=cosT.t[:wd, sl], op=ALU.mult),
             reads=[ps2[c], cosT], writes=[t4])
    P.op("dve", lambda e: e.tensor_tensor(out=o2.t[:wd, :], in0=t3.t[:wd, 0:TOK], in1=t4.t[:wd, 0:TOK], op=ALU.add),
         reads=[t3, t4], writes=[o2])
    P.dma("act", dst.t[r2:r2 + wd, :], o2.t[:wd, :], reads=[o2], writes=[dst.sub(("r", r2))], sem=o2)


def build_A0(nc, P):
    hT = P.dram("hT", [D, TOK], F32, kind="ExternalInput")
    cst_d = P.dram("cst", [128, A0_NCONST], F32, kind="ExternalInput")
    rope_d = P.dram("rope", [2, 128, TOK], F32, kind="ExternalInput")
    W = Weights(P, A0_WSPECS, A0_WSRCS)
    oF = P.dram("oF", [A0_OF_ROWS, TOK], F32, kind="ExternalOutput")
    oB = P.dram("oB", [A0_OB_ROWS, TOK], BF16, kind="ExternalOutput")
    scr = P.dram("scrA0", [1536, TOK], F32)
    R = Res(P, A0_NCONST)
    cosT = P.sb("cosT", [128, TOK])
    sinT = P.sb("sinT", [128, TOK])
    P.dma("sp", R.cst.t[:], cst_d.t[:, :], writes=[R.cst], sem=R.cst)
    P.dma("sp", cosT.t[:], rope_d.t[0], writes=[cosT], sem=cosT)
    P.dma("sp", sinT.t[:], rope_d.t[1], writes=[sinT], sem=sinT)

    norm_to_xT(R, hT, 32, 0)

    st_f = epi_store(R, oF, 0)
    st_b = epi_store(R, oB, 0, bf=True)
    st_s = epi_store(R, scr, 0)
    held = {}

    def epi_in(c0, wd, pset):
        if c0 < 4096:
            st_f(c0, wd, pset, c0)
        elif c0 < 6144:
            st_b(c0, wd, pset, c0 - 4096)
        elif c0 < 8192:
            st_f(c0, wd, pset, c0 - 6144 + 4096)
        elif c0 < 9728:
            st_s(c0, wd, pset, c0 - 8192)
        elif c0 == 9728:
            held["kr1"] = pset
        else:
            rope_epi(R, P, held["kr1"], pset, 32, cosT, sinT, oB, 9216, 9248)

    tiles = [(128 * i, 128) for i in range(76)] + [(9728, 32), (9760, 32)]
    tiles = [(W["in%d" % min(c0 // 2304, 4)], c0 - 2304 * min(c0 // 2304, 4), wd, c0) for (c0, wd) in tiles]
    panels = [tiles[i:i + 3] for i in range(0, 75, 3)] + [tiles[75:78]]
    gemm(R, 32, panels, epi_in)

    norm_to_xT(R, scr, 8, 32)

    def epi_uq(c0, wd, pset):
        if c0 < 2048:
            st_b(c0, wd, pset, 2048 + c0)
        elif c0 < 2560:
            held["x1"] = (pset, c0)
        else:
            j = (c0 - 2560) // 128
            rope_epi(R, P, held["x1"][0], pset, 128, cosT, sinT, oB, 4096 + 128 * j, 4608 + 128 * j)

    nope = [(W["uq"], 128 * h, 128, 128 * h) for h in range(16)]
    ropet = []
    for j in range(4):
        ropet += [(W["uq"], 2048 + 128 * j, 128, 2048 + 128 * j), (W["uq"], 2560 + 128 * j, 128, 2560 + 128 * j)]
    gemm(R, 8, [nope[0:12], nope[12:16], ropet], epi_uq)

    scr_kv = T(scr.t[1024:1536, :], "scrA0")
    scr_kv.subs = {k: scr.sub(8 + k) for k in range(4)}
    norm_to_xT(R, scr_kv, 4, 40)

    def epi_kv(c0, wd, pset):
        st_b(c0, wd, pset, 5120 + c0)

    kvt = [(W["ukv"], 128 * i, 128, 128 * i) for i in range(32)]
    gemm(R, 4, [kvt[0:24], kvt[24:32]], epi_kv)


def rope_tables_np(pos):
    inv = (10000.0 ** (-np.arange(0, 64, 2, dtype=np.float32) / np.float32(64))).astype(np.float32)
    ang = pos.astype(np.float32)[:, None] * inv[None, :]
    return np.cos(ang).astype(np.float32), np.sin(ang).astype(np.float32)


def col128(v, n):
    return np.ascontiguousarray(v.reshape(n, 128).T)


def host_A0(inp):
    x = inp["x"][0]
    h = np.concatenate([inp["meta_tokens"], x], axis=0)
    w_uq = inp["mla_w_uq"][0]
    idx = []
    for hh in range(16):
        idx += list(range(192 * hh, 192 * hh + 128))
    for half in range(2):
        for hh in range(16):
            idx += list(range(192 * hh + 128 + 32 * half, 192 * hh + 160 + 32 * half))
    w_uq_p = np.ascontiguousarray(w_uq[:, idx])
    cst = np.concatenate([col128(inp["norm_g"][0, 0], 32), col128(inp["mla_q_norm_g"][0], 8),
                          col128(inp["mla_kv_norm_g"][0], 4)], axis=1).astype(np.float32)
    maps = []
    for c in range(NCORE):
        hT = np.ascontiguousarray(h[1024 * c:1024 * c + TOK].T)
        cos, sin = rope_tables_np(np.arange(1024 * c, 1024 * c + TOK))
        rope = np.stack([np.tile(cos.T, (4, 1)), np.tile(sin.T, (4, 1))]).astype(np.float32)
        maps.append({"hT": hT, "cst": cst, "rope": rope, "w_in": inp["ab_w_in"][0], "w_uq": w_uq_p, "w_ukv": inp["mla_w_ukv"][0]})
    return maps


GC = 8
SCALE = 192 ** -0.5


def bcast_mid(ap2d, n):
    return ap2d.unsqueeze(2).broadcast_to([ap2d.shape[0], ap2d.shape[1], n])


def build_B0(nc, P, LPv=LP, do_hgrn=True, do_attn=True):
    NCH = LPv // 64
    hq = P.dram("hq", [2, 128, LPv], F32, kind="ExternalInput")
    hf = P.dram("hf", [2, 128, LPv], F32, kind="ExternalInput")
    hv = P.dram("hv", [2, 128, LPv], BF16, kind="ExternalInput")
    lbl = P.dram("lbl", [128, 2, 3], F32, kind="ExternalInput")
    aq = P.dram("aq", [2, 128, LPv], BF16, kind="ExternalInput")
    aqr = P.dram("aqr", [2, 64, LPv], BF16, kind="ExternalInput")
    ak = P.dram("ak", [2, 128, LPv], BF16, kind="ExternalInput")
    akr = P.dram("akr", [64, LPv], BF16, kind="ExternalInput")
    av = P.dram("av", [2, 128, LPv], BF16, kind="ExternalInput")
    cbf = P.dram("cbf", [128, 128 + 4 * 512 + 128 + 128], BF16, kind="ExternalInput")
    cf = P.dram("cf", [128, 64 + 1664], F32, kind="ExternalInput")
    oa = P.dram("oa", [2, 128, LPv], F32, kind="ExternalOutput")
    ob = P.dram("ob", [2, 128, LPv], BF16, kind="ExternalOutput")

    ps = [P.ps("ps%d" % i, [128, 512]) for i in range(8)]
    cb = P.sb("cb_sb", [128, 128 + 4 * 512 + 256], BF16)
    cfs = P.sb("cf_sb", [128, 64 + 1664], F32)
    P.dma("sp", cb.t[:], cbf.t[:, :], writes=[cb], sem=cb)
    P.dma("sp", cfs.t[:], cf.t[:, :], writes=[cfs], sem=cfs)
    ident = cb.t[:, 0:128]
    masks = cb.t[:, 128:128 + 2048].rearrange("p (r q) -> p r q", q=512)
    onesb = cb.t[:, 2176:2304]
    ones0 = cb.t[:, 2304:2432]
    triu = cfs.t[0:64, 0:64]

    if do_hgrn:
        BL = 1664 if LPv % 1664 == 0 else LPv
        NBLK = LPv // BL
        CPB = BL // 64
        rmask = cfs.t[:, 64:64 + BL]
        lb3 = P.sb("lb3", [128, 2, 3])
        lbe = P.sb("lbe", [128, 2, 3])
        lbs = P.sb("lbs", [128, 2])
        lb = P.sb("lb", [128, 2])
        oml = P.sb("oml", [128, 2])
        P.dma("sp", lb3.t[:], lbl.t[:, :, :], writes=[lb3], sem=lb3)
        P.op("act", lambda e: e.activation(out=lbe.t[:], in_=lb3.t[:], func=AF.Exp), reads=[lb3], writes=[lbe])
        P.op("dve", lambda e: e.tensor_tensor(out=lbs.t[:], in0=lbe.t[:, :, 0], in1=lbe.t[:, :, 1], op=ALU.add), reads=[lbe], writes=[lbs])
        P.op("dve", lambda e: e.tensor_tensor(out=lbs.t[:], in0=lbs.t[:], in1=lbe.t[:, :, 2], op=ALU.add), reads=[lbe, lbs], writes=[lbs])
        P.op("dve", lambda e: e.reciprocal(out=lbs.t[:], in_=lbs.t[:]), reads=[lbs], writes=[lbs])
        P.op("dve", lambda e: e.tensor_tensor(out=lb.t[:], in0=lbe.t[:, :, 0], in1=lbs.t[:], op=ALU.mult), reads=[lbe, lbs], writes=[lb])
        P.op("dve", lambda e: e.tensor_scalar(out=oml.t[:], in0=lb.t[:], scalar1=-1.0, scalar2=1.0, op0=ALU.mult, op1=ALU.add),
             reads=[lb], writes=[oml])

        def fb(name, dt=F32, n=2):
            return [P.sb("%s%d" % (name, i), [128, BL], dt) for i in range(n)]
        qin, fin, vin = fb("qin"), fb("fin"), fb("vin", BF16)
        kk, bb, eb, enb = fb("kk", n=1)[0], fb("bb", n=1)[0], fb("eb"), fb("enb", n=1)[0]
        qt, ktb, kh = fb("qt", BF16), fb("ktb", BF16), fb("kh", BF16)
        obuf = fb("obuf")
        vtok = [P.sb("vtok%d" % i, [64, GC, 128], BF16) for i in range(2)]
        khtok = [P.sb("khtok%d" % i, [64, GC, 128], BF16) for i in range(2)]
        atm = [P.sb("atm%d" % i, [64, GC, 64], BF16) for i in range(2)]
        S = P.sb("S", [128, 128])
        Sb = P.sb("Sb", [128, 128], BF16)
        it = 0
        gi = 0
        for hh in range(2):
            P.op("dve", lambda e: e.memset(S.t[:], 0.0), writes=[S])
            P.op("dve", lambda e: e.memset(Sb.t[:], 0.0), writes=[Sb])
            for blk in range(NBLK):
                i2 = it % 2
                it += 1
                sl = slice(blk * BL, (blk + 1) * BL)
                q_, f_, v_, eb_, qt_, ktb_, kh_, ob_ = qin[i2], fin[i2], vin[i2], eb[i2], qt[i2], ktb[i2], kh[i2], obuf[i2]
                P.dma("sp", q_.t[:], hq.t[hh, :, sl], writes=[q_], sem=q_)
                P.dma("sp", f_.t[:], hf.t[hh, :, sl], writes=[f_], sem=f_)
                P.dma("sp", v_.t[:], hv.t[hh, :, sl], writes=[v_], sem=v_)
                P.op("act", lambda e, f_=f_: e.activation(out=f_.t[:], in_=f_.t[:], func=AF.Sigmoid), reads=[f_], writes=[f_])
                P.op("dve", lambda e, f_=f_, hh=hh: e.tensor_scalar(out=f_.t[:], in0=f_.t[:], scalar1=oml.t[:, hh:hh + 1], scalar2=lb.t[:, hh:hh + 1],
                                                                    op0=ALU.mult, op1=ALU.add), reads=[f_, oml, lb], writes=[f_])
                P.op("dve", lambda e, f_=f_: e.tensor_scalar(out=kk.t[:], in0=f_.t[:], scalar1=-1.0, scalar2=1.0, op0=ALU.mult, op1=ALU.add),
                     reads=[f_], writes=[kk])
                P.op("act", lambda e, f_=f_: e.activation(out=f_.t[:], in_=f_.t[:], func=AF.Ln), reads=[f_], writes=[f_])
                P.op("dve", lambda e, f_=f_: e.tensor_tensor_scan(out=bb.t[:], data0=rmask, data1=f_.t[:], initial=0.0, op0=ALU.mult, op1=ALU.add),
                     reads=[f_, cfs], writes=[bb])
                P.op("act", lambda e, eb_=eb_: e.activation(out=eb_.t[:], in_=bb.t[:], func=AF.Exp), reads=[bb], writes=[eb_])
                P.op("act", lambda e: e.activation(out=enb.t[:], in_=bb.t[:], func=AF.Exp, scale=-1.0), reads=[bb], writes=[enb])
                P.op("dve", lambda e, q_=q_, eb_=eb_, qt_=qt_: e.tensor_tensor(out=qt_.t[:], in0=q_.t[:], in1=eb_.t[:], op=ALU.mult),
                     reads=[q_, eb_], writes=[qt_])
                P.op("dve", lambda e: e.tensor_tensor(out=kk.t[:], in0=kk.t[:], in1=enb.t[:], op=ALU.mult), reads=[kk, enb], writes=[kk])
                P.op("act", lambda e, ktb_=ktb_: e.activation(out=ktb_.t[:], in_=kk.t[:], func=AF.Copy), reads=[kk], writes=[ktb_])
                P.op("dve", lambda e, eb_=eb_, kh_=kh_: e.tensor_tensor(
                    out=kh_.t[:].rearrange("p (c t) -> p c t", t=64), in0=kk.t[:].rearrange("p (c t) -> p c t", t=64),
                    in1=bcast_mid(eb_.t[:, 63::64], 64), op=ALU.mult), reads=[kk, eb_], writes=[kh_])
                for g0 in range(0, CPB, GC):
                    gn = min(GC, CPB - g0)
                    g2 = gi % 2
                    gi += 1
                    vt_, kt_, at_ = vtok[g2], khtok[g2], atm[g2]
                    pv, pk, pa, po = ps[0 + g2], ps[2 + g2], ps[4], ps[5]
                    for c in range(gn):
                        cs = slice((g0 + c) * 64, (g0 + c + 1) * 64)
                        half = c // 4
                        P.op("pe", lambda e, c=c, cs=cs, v_=v_, pv=pv, pk=pk: e.matmul(
                            (pv if c < 4 else pk).t[0:64, (c % 4) * 128:(c % 4 + 1) * 128], lhsT=v_.t[:, cs], rhs=ident, start=True, stop=True),
                            reads=[v_, cb], writes=[pv if c < 4 else pk])
                    for half in range((gn + 3) // 4):
                        n4 = min(4, gn - half * 4)
                        src = (pv if half == 0 else pk)
                        P.op("act", lambda e, half=half, n4=n4, src=src, vt_=vt_: e.activation(
                            out=vt_.t[:, half * 4:half * 4 + n4, :], in_=src.t[0:64, 0:n4 * 128].rearrange("p (c d) -> p c d", d=128), func=AF.Copy),
                            reads=[src], writes=[vt_])
                    for c in range(gn):
                        cs = slice((g0 + c) * 64, (g0 + c + 1) * 64)
                        P.op("pe", lambda e, c=c, cs=cs, kh_=kh_, pv=pv, pk=pk: e.matmul(
                            (pv if c < 4 else pk).t[0:64, (c % 4) * 128:(c % 4 + 1) * 128], lhsT=kh_.t[:, cs], rhs=ident, start=True, stop=True),
                            reads=[kh_, cb], writes=[pv if c < 4 else pk])
                    for half in range((gn + 3) // 4):
                        n4 = min(4, gn - half * 4)
                        src = (pv if half == 0 else pk)
                        P.op("dve", lambda e, half=half, n4=n4, src=src, kt_=kt_: e.tensor_copy(
                            out=kt_.t[:, half * 4:half * 4 + n4, :], in_=src.t[0:64, 0:n4 * 128].rearrange("p (c d) -> p c d", d=128)),
                            reads=[src], writes=[kt_])
                    for c in range(gn):
                        cs = slice((g0 + c) * 64, (g0 + c + 1) * 64)
                        P.op("pe", lambda e, c=c, cs=cs, ktb_=ktb_, qt_=qt_: e.matmul(pa.t[0:64, c * 64:(c + 1) * 64], lhsT=ktb_.t[:, cs], rhs=qt_.t[:, cs],
                                                                                     start=True, stop=True), reads=[ktb_, qt_], writes=[pa])
                    P.op("dve", lambda e, gn=gn, at_=at_: e.tensor_tensor(
                        out=at_.t[:, 0:gn, :], in0=pa.t[0:64, 0:gn * 64].rearrange("p (c t) -> p c t", t=64),
                        in1=triu.unsqueeze(1).broadcast_to([64, gn, 64]), op=ALU.mult), reads=[pa, cfs], writes=[at_])
                    for c in range(gn):
                        cg = g0 + c
                        cs = slice(cg * 64, (cg + 1) * 64)
                        pn = ps[6 + (cg % 2)]
                        P.op("pe", lambda e, c=c, cs=cs, qt_=qt_: e.matmul(po.t[:, c * 64:(c + 1) * 64], lhsT=Sb.t[:], rhs=qt_.t[:, cs], start=True, stop=False),
                             reads=[Sb, qt_], writes=[po])
                        P.op("pe", lambda e, c=c, vt_=vt_, at_=at_: e.matmul(po.t[:, c * 64:(c + 1) * 64], lhsT=vt_.t[:, c, :], rhs=at_.t[:, c, :], start=False, stop=True),
                             reads=[vt_, at_], writes=[po])
                        P.op("pe", lambda e, c=c, kt_=kt_, vt_=vt_, pn=pn: e.matmul(pn.t[:, 0:128], lhsT=kt_.t[:, c, :], rhs=vt_.t[:, c, :], start=True, stop=True),
                             reads=[kt_, vt_], writes=[pn])
                        P.op("dve", lambda e, cg=cg, eb_=eb_, pn=pn: e.scalar_tensor_tensor(out=Sb.t[:], in0=S.t[:], scalar=eb_.t[:, cg * 64 + 63:cg * 64 + 64],
                                                                                          in1=pn.t[:, 0:128], op0=ALU.mult, op1=ALU.add),
                             reads=[S, eb_, pn], writes=[Sb])
                        P.op("dve", lambda e, cg=cg, eb_=eb_, pn=pn: e.scalar_tensor_tensor(out=S.t[:], in0=S.t[:], scalar=eb_.t[:, cg * 64 + 63:cg * 64 + 64],
                                                                                          in1=pn.t[:, 0:128], op0=ALU.mult, op1=ALU.add),
                             reads=[S, eb_, pn], writes=[S])
                    P.op("act", lambda e, g0=g0, gn=gn, ob_=ob_: e.activation(out=ob_.t[:, g0 * 64:(g0 + gn) * 64], in_=po.t[:, 0:gn * 64], func=AF.Copy),
                         reads=[po], writes=[ob_])
                P.dma("sp", oa.t[hh, :, sl], ob_.t[:], reads=[ob_], writes=[oa.sub((hh, blk))], sem=ob_)

    if do_attn:
        NKT = LPv // 128
        kn = P.sb("kn", [128, LPv], BF16)
        krr = P.sb("krr", [64, LPv], BF16)
        qnb = [P.sb("qn%d" % i, [128, 512], BF16) for i in range(2)]
        qrb = [P.sb("qr%d" % i, [64, 512], BF16) for i in range(2)]
        vT = P.sb("vT", [128, LPv], BF16)
        vk = P.sb("vk", [128, NKT, 128], BF16)
        pT = [P.sb("pT%d" % i, [128, 512], BF16) for i in range(3)]
        ost = [P.sb("ost%d" % i, [128, 512], BF16) for i in range(2)]
        rden = P.sb("rden", [128, 512])
        P.dma("sp", krr.t[:], akr.t[:, :], writes=[krr], sem=krr)
        pi = 0
        oi = 0
        for hh in range(2):
            P.dma("sp", kn.t[:], ak.t[hh], writes=[kn], sem=kn)
            P.dma("sp", vT.t[:], av.t[hh], writes=[vT], sem=vT)
            for j0 in range(0, NKT, 4):
                jn = min(4, NKT - j0)
                pv = ps[(j0 // 4) % 2]
                for j in range(jn):
                    P.op("pe", lambda e, j=j, j0=j0, pv=pv: e.matmul(pv.t[:, j * 128:(j + 1) * 128], lhsT=vT.t[:, (j0 + j) * 128:(j0 + j + 1) * 128],
                                                                   rhs=ident, start=True, stop=True), reads=[vT, cb], writes=[pv])
                P.op("dve", lambda e, j0=j0, jn=jn, pv=pv: e.tensor_copy(out=vk.t[:, j0:j0 + jn, :],
                                                                       in_=pv.t[:, 0:jn * 128].rearrange("p (c d) -> p c d", d=128)),
                     reads=[pv], writes=[vk])
            nq = (LPv + 511) // 512
            for qi in range(nq):
                q0 = qi * 512
                qw = min(512, LPv - q0)
                jmax = min(NKT - 1, 4 * qi + 3)
                po, pd = ps[4 + 2 * (qi % 2)], ps[5 + 2 * (qi % 2)]
                qn, qr = qnb[qi % 2], qrb[qi % 2]
                P.dma("sp", qn.t[:, 0:qw], aq.t[hh, :, q0:q0 + qw], writes=[qn], sem=qn)
                P.dma("sp", qr.t[:, 0:qw], aqr.t[hh, :, q0:q0 + qw], writes=[qr], sem=qr)
                def qk(j):
                    pss_ = ps[j % 2]
                    ks = slice(j * 128, (j + 1) * 128)
                    P.op("pe", lambda e: e.matmul(pss_.t[:, 0:qw], lhsT=kn.t[:, ks], rhs=qn.t[:, 0:qw], start=True, stop=False),
                         reads=[kn, qn], writes=[pss_])
                    P.op("pe", lambda e: e.matmul(pss_.t[:, 0:qw], lhsT=krr.t[:, ks], rhs=qr.t[:, 0:qw], start=False, stop=True),
                         reads=[krr, qr], writes=[pss_])
                qk(0)
                for j in range(jmax + 1):
                    pss_ = ps[j % 2]
                    if j + 1 <= jmax:
                        qk(j + 1)
                    p_ = pT[pi % 3]
                    pi += 1
                    P.op("act", lambda e, qw=qw, pss_=pss_, p_=p_: e.activation(out=p_.t[:, 0:qw], in_=pss_.t[:, 0:qw], func=AF.Exp, scale=SCALE),
                         reads=[pss_], writes=[p_])
                    r = j - 4 * qi
                    if r >= 0:
                        P.op("dve", lambda e, qw=qw, r=r, p_=p_: e.tensor_tensor(out=p_.t[:, 0:qw], in0=p_.t[:, 0:qw], in1=masks[:, r, 0:qw], op=ALU.mult),
                             reads=[p_, cb], writes=[p_])
                    P.op("pe", lambda e, j=j, qw=qw, p_=p_, jmax=jmax, po=po: e.matmul(po.t[:, 0:qw], lhsT=vk.t[:, j, :], rhs=p_.t[:, 0:qw], start=(j == 0), stop=(j == jmax)),
                         reads=[vk, p_], writes=[po])
                    P.op("pe", lambda e, j=j, qw=qw, p_=p_, jmax=jmax, pd=pd: e.matmul(pd.t[:, 0:qw], lhsT=(ones0 if j == 0 else onesb), rhs=p_.t[:, 0:qw], start=(j == 0), stop=(j == jmax)),
                         reads=[cb, p_], writes=[pd])
                o_ = ost[oi % 2]
                oi += 1
                P.op("dve", lambda e, qw=qw, pd=pd: e.tensor_scalar(out=rden.t[:, 0:qw], in0=pd.t[:, 0:qw], scalar1=1e-30, scalar2=None, op0=ALU.max),
                     reads=[pd], writes=[rden])
                P.op("dve", lambda e, qw=qw: e.reciprocal(out=rden.t[:, 0:qw], in_=rden.t[:, 0:qw]), reads=[rden], writes=[rden])
                P.op("dve", lambda e, qw=qw, po=po, o_=o_: e.tensor_tensor(out=o_.t[:, 0:qw], in0=po.t[:, 0:qw], in1=rden.t[:, 0:qw], op=ALU.mult),
                     reads=[po, rden], writes=[o_])
                P.dma("sp", ob.t[hh, :, q0:q0 + qw], o_.t[:, 0:qw], reads=[o_], writes=[ob.sub((hh, qi))], sem=o_)


def consts_B0():
    cbf = np.zeros((128, 128 + 2048 + 256), np.float32)
    cbf[:, 0:128] = np.eye(128)
    k = np.arange(128)[:, None]
    q = np.arange(512)[None, :]
    for r in range(4):
        cbf[:, 128 + 512 * r:128 + 512 * (r + 1)] = ((2 * r + (k >= 64)) <= (q // 64)).astype(np.float32)
    cbf[:, 2176:2304] = 1.0
    cbf[PADL:, 2304:2432] = 1.0
    cf = np.zeros((128, 64 + 1664), np.float32)
    s = np.arange(64)[:, None]
    t = np.arange(64)[None, :]
    cf[0:64, 0:64] = (s <= t).astype(np.float32)
    rm = np.ones(1664, np.float32)
    rm[0::64] = 0.0
    cf[:, 64:] = rm[None, :]
    return cbf.astype(NPBF), cf


def assemble(outs):
    rows = outs[0].shape[0]
    full = np.zeros((rows, LP), outs[0].dtype)
    full[:, PADL:PADL + TOK] = outs[0]
    for c in range(1, NCORE):
        full[:, PADL + 1024 * c + 16:PADL + 1024 * c + TOK] = outs[c][:, 16:]
    return full


def host_B0(inp, oF, oB, LPv=LP):
    fF = assemble(oF)[:, :LPv]
    fB = assemble(oB)[:, :LPv]
    cbf, cf = consts_B0()
    lbl = inp["lb_logits"]
    maps = []
    for c in range(NCORE):
        hs = [2 * c, 2 * c + 1]
        m = {}
        m["hq"] = np.stack([fF[128 * h:128 * h + 128] for h in hs])
        m["hf"] = np.stack([fF[2048 + 128 * h:2048 + 128 * h + 128] for h in hs])
        m["hv"] = np.stack([fB[128 * h:128 * h + 128] for h in hs])
        m["lbl"] = np.ascontiguousarray(np.stack([lbl[:, 128 * h:128 * h + 128].T for h in hs], axis=1))
        m["aq"] = np.stack([fB[2048 + 128 * h:2048 + 128 * h + 128] for h in hs])
        qr = []
        for h in hs:
            j, hh = h // 4, h % 4
            qr.append(np.concatenate([fB[4096 + 128 * j + 32 * hh:4096 + 128 * j + 32 * hh + 32],
                                      fB[4608 + 128 * j + 32 * hh:4608 + 128 * j + 32 * hh + 32]], axis=0))
        m["aqr"] = np.stack(qr)
        m["ak"] = np.stack([fB[5120 + 256 * h:5120 + 256 * h + 128] for h in hs])
        m["av"] = np.stack([fB[5120 + 256 * h + 128:5120 + 256 * h + 256] for h in hs])
        m["akr"] = np.ascontiguousarray(fB[9216:9280])
        m["cbf"] = cbf
        m["cf"] = cf
        maps.append({k: np.ascontiguousarray(v) for k, v in m.items()})
    return maps


FFN_G = [15, 15, 14, 14, 14, 14]
NT_UP = 172

C_G1, C_G2, C_G3, C_GN, C_CW, C_CB, C_X = 0, 32, 64, 96, 128, 128 + 516, 128 + 516 + 172
C_NBASE = C_X


def ffn(R, P, W, ffT, aT, U, Y, Gt):
    c0g = 0
    for gidx, gsz in enumerate(FFN_G):
        held = {}

        def conv(pset, tile, ui):
            u = U[ui % 2]
            y = Y[ui % 2]
            for c, (t0, tl) in enumerate(CH):
                P.op("act", lambda e, c=c, t0=t0, tl=tl: e.activation(out=u.t[:, 2 + t0:2 + t0 + tl], in_=pset[c].t[:, :tl], func=AF.Copy),
                     reads=[pset[c]], writes=[u])
            cw = C_CW + 3 * tile
            P.op("act", lambda e: e.activation(out=y.t[:], in_=u.t[:, 2:2 + TOK], func=AF.Identity,
                                               scale=R.cst.t[:, cw + 2:cw + 3], bias=R.cst.t[:, C_CB + tile:C_CB + tile + 1]),
                 reads=[u, R.cst], writes=[y])
            P.op("dve", lambda e: e.scalar_tensor_tensor(out=y.t[:], in0=u.t[:, 1:1 + TOK], scalar=R.cst.t[:, cw + 1:cw + 2], in1=y.t[:],
                                                         op0=ALU.mult, op1=ALU.add), reads=[u, y, R.cst], writes=[y])
            P.op("dve", lambda e: e.scalar_tensor_tensor(out=y.t[:], in0=u.t[:, 0:TOK], scalar=R.cst.t[:, cw:cw + 1], in1=y.t[:],
                                                         op0=ALU.mult, op1=ALU.add), reads=[u, y, R.cst], writes=[y])
            return y

        ui = [0]

        def epi_up(c0, wd, pset):
            tile = c0 // 128
            y = conv(pset, tile, ui[0])
            ui[0] += 1
            if c0 < DFF:
                P.op("act", lambda e: e.activation(out=Gt.t[:], in_=y.t[:], func=AF.Silu), reads=[y], writes=[Gt])
            else:
                jj = (c0 - DFF) // 128 - c0g
                P.op("dve", lambda e: e.tensor_tensor(out=aT.t[:, jj, :], in0=Gt.t[:], in1=y.t[:], op=ALU.mult), reads=[Gt, y], writes=[aT])

        panels = [[(W["upg%d" % gidx], 128 * (j - c0g), 128, 128 * j), (W["upv%d" % gidx], 128 * (j - c0g), 128, DFF + 128 * j)]
                  for j in range(c0g, c0g + gsz)]
        gemm(R, 32, panels, epi_up)

        last = gidx == len(FFN_G) - 1
        if last:
            sumsq_begin(R)

        def epi_dn(c0, wd, pset):
            s = R.next_stg()
            kt = c0 // 128
            if gidx == 0:
                evac(R, pset, wd, lambda t0, tl: s.t[:wd, t0:t0 + tl], s)
            else:
                p = R.next_stg()
                P.dma("sp", p.t[:, 0:TOK], ffT.t[c0:c0 + 128, :], reads=[ffT.sub(kt)], writes=[p], sem=p)
                for c, (t0, tl) in enumerate(CH):
                    P.op("dve", lambda e, c=c, t0=t0, tl=tl: e.tensor_tensor(out=s.t[:, t0:t0 + tl], in0=pset[c].t[:, :tl], in1=p.t[:, t0:t0 + tl], op=ALU.add),
                         reads=[pset[c], p], writes=[s])
            P.dma("act", ffT.t[c0:c0 + 128, :], s.t[:, 0:TOK], reads=[s], writes=[ffT.sub(kt)], sem=s)
            if last:
                sumsq_add(R, s.t[:, 0:TOK], [s])

        tiles = [(W["dn%d" % gidx], 128 * i, 128, 128 * i) for i in range(32)]
        gemm(R, gsz, [tiles[i:i + 4] for i in range(0, 32, 4)], epi_dn, xT=aT)
        c0g += gsz
    sumsq_finish(R, D, R.pss[0])


def build_C(nc, P, layer):
    hT = P.dram("hT", [D, TOK], F32, kind="ExternalInput")
    cst_d = P.dram("cst", [128, C_NBASE + 64], F32, kind="ExternalInput")
    W = Weights(P, c_wspecs(layer), c_wsrcs(layer))
    mixT = P.dram("mixT", [D, TOK], F32)
    hmid = P.dram("hmid", [D, TOK], F32)
    ffT = P.dram("ffT", [D, TOK], F32)
    R = Res(P, C_NBASE + 64)
    aT = P.sb("aT", [128, max(FFN_G), TOK], BF16)
    U = [P.sb("U%d" % i, [128, TOK + 2]) for i in range(2)]
    Y = [P.sb("Y%d" % i, [128, TOK]) for i in range(2)]
    Gt = P.sb("Gt", [128, TOK])
    for u in U:
        P.op("dve", lambda e, u=u: e.memset(u.t[:, 0:2], 0.0), writes=[u])
    P.dma("sp", R.cst.t[:], cst_d.t[:, :], writes=[R.cst], sem=R.cst)

    if layer == 0:
        oaT = P.dram("oaT", [2048, TOK], F32, kind="ExternalInput")
        obT = P.dram("obT", [2048, TOK], BF16, kind="ExternalInput")
        gaT = P.dram("gaT", [2048, TOK], F32, kind="ExternalInput")
        h1T = P.dram("h1T", [D, TOK], F32, kind="ExternalOutput")
        pF1 = P.dram("pF1", [9264, TOK], F32, kind="ExternalOutput")
        for h in range(16):
            a = R.next_stg()
            g = R.next_stg()
            rows = slice(128 * h, 128 * h + 128)
            P.dma("sp", a.t[:, 0:TOK], oaT.t[rows, :], writes=[a], sem=a)
            P.dma("sp", g.t[:, 0:TOK], gaT.t[rows, :], writes=[g], sem=g)
            sumsq_begin(R)
            sumsq_add(R, a.t[:, 0:TOK], [a])
            sumsq_finish(R, 128, R.pss[h % 2])
            P.op("act", lambda e, g=g: e.activation(out=g.t[:, 0:TOK], in_=g.t[:, 0:TOK], func=AF.Silu), reads=[g], writes=[g])
            P.op("dve", lambda e, a=a, h=h: e.scalar_tensor_tensor(out=a.t[:, 0:TOK], in0=a.t[:, 0:TOK], scalar=R.cst.t[:, C_X + h:C_X + h + 1],
                                                                  in1=R.rstd.t[:], op0=ALU.mult, op1=ALU.mult), reads=[a, R.rstd, R.cst], writes=[a])
            P.op("dve", lambda e, a=a, g=g, h=h: e.tensor_tensor(out=R.xT.t[:, h, :], in0=a.t[:, 0:TOK], in1=g.t[:, 0:TOK], op=ALU.mult),
                 reads=[a, g], writes=[R.xT])
        for h in range(16):
            P.dma("sp", R.xT.t[:, 16 + h, :], obT.t[128 * h:128 * h + 128, :], writes=[R.xT], sem=R.xT)
    else:
        plT = P.dram("plT", [1024, TOK], F32, kind="ExternalInput")
        ysT = P.dram("ysT", [3072, TOK], F32, kind="ExternalInput")
        zT = P.dram("zT", [3072, TOK], F32, kind="ExternalInput")
        h2T = P.dram("h2T", [D, TOK], F32, kind="ExternalOutput")
        yzT = P.dram("yzT", [3072, TOK], F32)
        for kt in range(8):
            a = R.next_stg()
            P.dma("sp", a.t[:, 0:TOK], plT.t[128 * kt:128 * kt + 128, :], writes=[a], sem=a)
            P.op("dve", lambda e, a=a, kt=kt: e.tensor_copy(out=aT.t[:, kt, :], in_=a.t[:, 0:TOK]), reads=[a], writes=[aT])
        for g in range(4):
            def epi_pool(c0, wd, pset, g=g):
                j = 2 * g + c0 // 128
                for c, (t0, tl) in enumerate(CH):
                    P.op("act", lambda e, c=c, t0=t0, tl=tl: e.activation(out=R.xT.t[:, j, t0:t0 + tl], in_=pset[c].t[:, :tl], func=AF.Identity,
                                                                         scale=R.cst.t[:, C_X + j:C_X + j + 1]), reads=[pset[c], R.cst], writes=[R.xT])
            pwg = T(W["poolw"].t[256 * g:256 * g + 256, :], "pool_w%d" % g)
            pwg.b = W["poolw"].b
            gemm(R, 2, [[(pwg, 0, 128, 0), (pwg, 128, 128, 128)]], epi_pool, xT=aT, kt0=2 * g)
        sumsq_begin(R)
        for kt in range(24):
            a = R.next_stg()
            b = R.next_stg()
            rows = slice(128 * kt, 128 * kt + 128)
            P.dma("sp", a.t[:, 0:TOK], ysT.t[rows, :], writes=[a], sem=a)
            P.dma("sp", b.t[:, 0:TOK], zT.t[rows, :], writes=[b], sem=b)
            P.op("act", lambda e, b=b: e.activation(out=b.t[:, 0:TOK], in_=b.t[:, 0:TOK], func=AF.Silu), reads=[b], writes=[b])
            P.op("dve", lambda e, a=a, b=b: e.tensor_tensor(out=a.t[:, 0:TOK], in0=a.t[:, 0:TOK], in1=b.t[:, 0:TOK], op=ALU.mult), reads=[a, b], writes=[a])
            P.dma("act", yzT.t[rows, :], a.t[:, 0:TOK], reads=[a], writes=[yzT.sub(kt)], sem=a)
            sumsq_add(R, a.t[:, 0:TOK], [a])
        sumsq_finish(R, 3072, R.pss[0])
        apply_norm(R, yzT, 24, C_X + 8, kt_off=8)

    sumsq_begin(R)

    def epi_out(c0, wd, pset):
        s = R.next_stg()
        evac(R, pset, wd, lambda t0, tl: s.t[:wd, t0:t0 + tl], s)
        P.dma("act", mixT.t[c0:c0 + wd, :], s.t[:wd, 0:TOK], reads=[s], writes=[mixT.sub(c0 // 128)], sem=s)
        sumsq_add(R, s.t[:, 0:TOK], [s])

    tiles = [(W["out"], 128 * i, 128, 128 * i) for i in range(32)]
    gemm(R, 32, [tiles[i:i + 3] for i in range(0, 32, 3)], epi_out)
    sumsq_finish(R, D, R.pss[0])
    norm_residual(R, mixT, hT, hmid, C_G1)
    apply_norm(R, hmid, 32, C_G2)
    ffn(R, P, W, ffT, aT, U, Y, Gt)
    if layer == 1:
        norm_residual(R, ffT, hmid, h2T, C_G3)
    if layer == 0:
        norm_residual(R, ffT, hmid, h1T, C_G3)
        apply_norm(R, h1T, 32, C_GN)
        st = epi_store(R, pF1, 0)
        tl2 = [(128 * i, 128) for i in range(72)] + [(9216, 48)]
        tl2 = [(W["nin%d" % min(c0 // 2304, 3)], c0 - 2304 * min(c0 // 2304, 3), wd, c0) for (c0, wd) in tl2]
        gemm(R, 32, [tl2[i:i + 3] for i in range(0, 73, 3)], lambda c0, wd, pset: st(c0, wd, pset, c0))


def c_wspecs(layer):
    sp = ([("poolw", "pool_w", 0, 1024, 0, 256)] if layer == 1 else []) + [("out", "w_out", 0, D, 0, D)]
    c0 = 0
    for g, gsz in enumerate(FFN_G):
        sp += [("upg%d" % g, "w_up", 0, D, 128 * c0, 128 * gsz), ("upv%d" % g, "w_up", 0, D, DFF + 128 * c0, 128 * gsz),
               ("dn%d" % g, "w_down", 128 * c0, 128 * gsz, 0, D)]
        c0 += gsz
    if layer == 0:
        sp += [("nin%d" % i, "w_in2", 0, D, 2304 * i, 2304) for i in range(3)] + [("nin3", "w_in2", 0, D, 6912, 9264 - 6912)]
    return sp


def c_wsrcs(layer):
    d = {"w_out": (D, D), "w_up": (D, 2 * DFF), "w_down": (DFF, D)}
    if layer == 0:
        d["w_in2"] = (D, 9264)
    else:
        d["pool_w"] = (1024, 256)
    return d


def host_C_consts(inp, layer):
    ng = inp["norm_g"][layer]
    cw = inp["ffn_conv_w"][layer]
    cwt = np.ascontiguousarray(cw.reshape(3, NT_UP, 128).transpose(2, 1, 0)).reshape(128, NT_UP * 3)
    cb = col128(inp["ffn_conv_b"][layer], NT_UP)
    gn = col128(inp["norm_g"][layer + 1, 0], 32) if layer + 1 < 2 else np.zeros((128, 32), np.float32)
    parts = [col128(ng[1], 32), col128(ng[2], 32), col128(ng[3], 32), gn, cwt, cb]
    x = np.zeros((128, 64), np.float32)
    if layer == 0:
        x[:, 0:16] = inp["hgrn_norm_g"][0].T
    else:
        x[:, 0:8] = col128(inp["pool_scale"][0], 8)
        x[:, 8:32] = col128(inp["ssm_norm_g"][0], 24)
    parts.append(x)
    return np.concatenate(parts, axis=1).astype(np.float32)


def col128(v, n):
    return np.ascontiguousarray(v.reshape(n, 128).T)


def strip_cols(full, c):
    return np.ascontiguousarray(full[:, PADL + 1024 * c:PADL + 1024 * c + TOK])


def host_C0(inp, oF_A0, oa_B0, ob_B0):
    h = np.concatenate([inp["meta_tokens"], inp["x"][0]], axis=0)
    oa_full = np.concatenate([o.reshape(256, LP) for o in oa_B0], axis=0)
    ob_full = np.concatenate([o.reshape(256, LP) for o in ob_B0], axis=0)
    cst = host_C_consts(inp, 0)
    maps = []
    for c in range(NCORE):
        maps.append({"hT": np.ascontiguousarray(h[1024 * c:1024 * c + TOK].T), "cst": cst, "w_out": inp["ab_w_out"][0],
                     "w_up": inp["ffn_w_up"][0], "w_down": inp["ffn_w_down"][0], "w_in2": inp["cd_w_in"][0],
                     "oaT": strip_cols(oa_full, c), "obT": strip_cols(ob_full, c), "gaT": np.ascontiguousarray(oF_A0[c][4096:6144])})
    return maps


def host_C1(inp, pF1_C0, h1T_C0, pooled_B1, y_B1):
    pl_full = np.concatenate(pooled_B1, axis=0)
    ys_full = np.concatenate([y.reshape(384, LP) for y in y_B1], axis=0)
    cst = host_C_consts(inp, 1)
    pool_w2 = np.ascontiguousarray(inp["pool_w"][0].reshape(1024, 256))
    maps = []
    for c in range(NCORE):
        maps.append({"hT": h1T_C0[c], "cst": cst, "w_out": inp["cd_w_out"][0], "w_up": inp["ffn_w_up"][1], "w_down": inp["ffn_w_down"][1],
                     "pool_w": pool_w2,
                     "plT": strip_cols(pl_full, c), "ysT": strip_cols(ys_full, c), "zT": np.ascontiguousarray(pF1_C0[c][1024:4096]),
                     })
    return maps


BL1 = 1664
K_ID, K_TRI, K_RM, K_CW, K_CB, K_SELW, K_CORR, K_D, K_DTA, K_E = 0, 128, 256, 256 + BL1, 256 + BL1 + 20, 256 + BL1 + 25, \
    256 + BL1 + 29, 256 + BL1 + 45, 256 + BL1 + 45 + 384, 256 + BL1 + 45 + 384 + 2
K_N = K_E + 768


def bc3(ap2d, n):
    return ap2d.unsqueeze(2).broadcast_to([ap2d.shape[0], ap2d.shape[1], n])


def build_B1(nc, P, LPv=LP):
    NBLK = LPv // BL1
    CPB = BL1 // 128
    pu = P.dram("pu", [128, LPv], F32, kind="ExternalInput")
    xin = P.dram("xin", [5, 128, LPv], F32, kind="ExternalInput")
    dtr = P.dram("dtr", [6, LPv], F32, kind="ExternalInput")
    kf = P.dram("kf", [128, K_N], F32, kind="ExternalInput")
    kb = P.dram("kb", [128, 128], BF16, kind="ExternalInput")
    pooled = P.dram("pooled", [128, LPv], F32, kind="ExternalOutput")
    yT = P.dram("yT", [3, 128, LPv], F32, kind="ExternalOutput")

    ps = [P.ps("ps%d" % i, [128, 512]) for i in range(8)]
    K = P.sb("kf_sb", [128, K_N])
    KB = P.sb("kb_sb", [128, 128], BF16)
    P.dma("sp", K.t[:], kf.t[:, :], writes=[K], sem=K)
    P.dma("sp", KB.t[:], kb.t[:, :], writes=[KB], sem=KB)
    identf = K.t[:, K_ID:K_ID + 128]
    triu = K.t[:, K_TRI:K_TRI + 128]
    identb = KB.t[:, :]

    HL = 16
    ub = [P.sb("ub%d" % i, [128, HL + BL1]) for i in range(2)]
    s2 = P.sb("s2", [128, HL + BL1])
    s4 = P.sb("s4", [128, HL + BL1])
    s8 = P.sb("s8", [128, HL + BL1])
    s16 = P.sb("s16", [128, HL + BL1])
    pm = [P.sb("pm%d" % i, [128, BL1]) for i in range(2)]
    for t in (s2, s4, s8, s16):
        P.op("dve", lambda e, t=t: e.memset(t.t[:], 0.0), writes=[t])
    for blk in range(NBLK):
        u = ub[blk % 2]
        m = pm[blk % 2]
        if blk == 0:
            P.op("dve", lambda e, u=u: e.memset(u.t[:, 0:HL], 0.0), writes=[u])
            P.dma("sp", u.t[:, HL:], pu.t[:, 0:BL1], writes=[u], sem=u)
        else:
            P.dma("sp", u.t[:, :], pu.t[:, blk * BL1 - HL:(blk + 1) * BL1], writes=[u], sem=u)
        W = HL + BL1
        for (dst, src, k) in ((s2, u, 1), (s4, s2, 2), (s8, s4, 4), (s16, s8, 8)):
            P.op("dve", lambda e, dst=dst, src=src, k=k: e.tensor_tensor(out=dst.t[:, k:W], in0=src.t[:, k:W], in1=src.t[:, 0:W - k], op=ALU.add),
                 reads=[src], writes=[dst])
        P.op("dve", lambda e, m=m: e.tensor_scalar(out=m.t[:], in0=s2.t[:, HL:], scalar1=K.t[:, K_SELW:K_SELW + 1], scalar2=None, op0=ALU.mult),
             reads=[s2, K], writes=[m])
        for wi, sw in ((1, s4), (2, s8), (3, s16)):
            P.op("dve", lambda e, m=m, wi=wi, sw=sw: e.scalar_tensor_tensor(out=m.t[:], in0=sw.t[:, HL:], scalar=K.t[:, K_SELW + wi:K_SELW + wi + 1],
                                                                           in1=m.t[:], op0=ALU.mult, op1=ALU.add), reads=[sw, m, K], writes=[m])
        if blk == 0:
            P.op("dve", lambda e, m=m: e.tensor_tensor(out=m.t[:, PADL:PADL + 16], in0=m.t[:, PADL:PADL + 16], in1=K.t[:, K_CORR:K_CORR + 16], op=ALU.mult),
                 reads=[m, K], writes=[m])
        P.op("dve", lambda e, m=m, u=u: e.tensor_tensor(out=m.t[:], in0=m.t[:], in1=u.t[:, HL:], op=ALU.subtract), reads=[m, u], writes=[m])
        P.dma("sp", pooled.t[:, blk * BL1:(blk + 1) * BL1], m.t[:], reads=[m], writes=[pooled.sub(blk)], sem=m)

    HC = 3
    raw = [P.sb("raw%d" % i, [128, HC + BL1]) for i in range(5)]
    cv = [P.sb("cv%d" % i, [128, BL1]) for i in range(5)]
    BTb = P.sb("BTb", [128, BL1], BF16)
    CTb = P.sb("CTb", [128, BL1], BF16)
    dtt = P.sb("dtt", [6, BL1])
    lat = P.sb("lat", [6, BL1])
    acst = P.sb("acst", [6, BL1])
    avec = P.sb("avec", [6, 1])
    yTb = [P.sb("yTb%d" % i, [128, BL1]) for i in range(3)]
    H = P.sb("H", [128, 384])
    Hb = P.sb("Hb", [128, 384], BF16)
    P.op("dve", lambda e: e.memset(H.t[:], 0.0), writes=[H])
    P.op("dve", lambda e: e.memset(Hb.t[:], 0.0), writes=[Hb])
    P.op("act", lambda e: e.activation(out=avec.t[:], in_=K.t[0:6, K_DTA + 1:K_DTA + 2], func=AF.Exp), reads=[K], writes=[avec])
    P.op("dve", lambda e: e.tensor_scalar(out=avec.t[:], in0=avec.t[:], scalar1=-1.0, scalar2=None, op0=ALU.mult), reads=[avec], writes=[avec])
    cbm = P.sb("cbm", [128, 128])
    xstok = P.sb("xstok", [128, 384])
    btok = P.sb("btok", [128, 128], BF16)
    dttok = P.sb("dttok", [128, 6])
    acstok = P.sb("acstok", [128, 6])
    diff = P.sb("diff", [128, 768])
    Gt = P.sb("Gt", [128, 768], BF16)
    xd = P.sb("xd", [128, 384])
    xdb = P.sb("xdb", [128, 384], BF16)
    xddb = P.sb("xddb", [128, 384], BF16)
    dec = P.sb("dec", [128, 6])
    eal = P.sb("eal", [128, 6])
    ea = P.sb("ea", [128, 6])
    t1 = P.sb("t1", [128, 384])
    t2 = P.sb("t2", [128, 384])
    onec = P.sb("onec", [128, 1])
    P.op("dve", lambda e: e.memset(onec.t[:], 1.0), writes=[onec])

    for blk in range(NBLK):
        b0 = blk * BL1
        for i in range(5):
            r = raw[i]
            if blk == 0:
                P.op("dve", lambda e, r=r: e.memset(r.t[:, 0:HC], 0.0), writes=[r])
                P.dma("sp", r.t[:, HC:], xin.t[i, :, 0:BL1], writes=[r], sem=r)
            else:
                P.dma("sp", r.t[:, :], xin.t[i, :, b0 - HC:b0 + BL1], writes=[r], sem=r)
            c = cv[i]
            cw = K_CW + 4 * i
            P.op("act", lambda e, r=r, c=c, cw=cw, i=i: e.activation(out=c.t[:], in_=r.t[:, 3:3 + BL1], func=AF.Identity,
                                                                    scale=K.t[:, cw + 3:cw + 4], bias=K.t[:, K_CB + i:K_CB + i + 1]),
                 reads=[r, K], writes=[c])
            for k in range(3):
                P.op("dve", lambda e, r=r, c=c, cw=cw, k=k: e.scalar_tensor_tensor(out=c.t[:], in0=r.t[:, k:k + BL1], scalar=K.t[:, cw + k:cw + k + 1],
                                                                                 in1=c.t[:], op0=ALU.mult, op1=ALU.add), reads=[r, c, K], writes=[c])
            P.op("act", lambda e, c=c: e.activation(out=c.t[:], in_=c.t[:], func=AF.Silu), reads=[c], writes=[c])
        P.op("dve", lambda e: e.tensor_copy(out=BTb.t[:], in_=cv[3].t[:]), reads=[cv[3]], writes=[BTb])
        P.op("dve", lambda e: e.tensor_copy(out=CTb.t[:], in_=cv[4].t[:]), reads=[cv[4]], writes=[CTb])
        P.dma("sp", dtt.t[:], dtr.t[:, b0:b0 + BL1], writes=[dtt], sem=dtt)
        P.op("act", lambda e: e.activation(out=dtt.t[:], in_=dtt.t[:], func=AF.Exp, bias=K.t[0:6, K_DTA:K_DTA + 1]), reads=[dtt, K], writes=[dtt])
        P.op("act", lambda e: e.activation(out=dtt.t[:], in_=dtt.t[:], func=AF.Ln, bias=onec.t[0:6, :]), reads=[dtt, onec], writes=[dtt])
        P.op("dve", lambda e: e.tensor_scalar(out=lat.t[:], in0=dtt.t[:], scalar1=avec.t[:, 0:1], scalar2=None, op0=ALU.mult), reads=[dtt, avec], writes=[lat])
        P.op("dve", lambda e: e.tensor_tensor_scan(out=acst.t[:], data0=K.t[0:6, K_RM:K_RM + BL1], data1=lat.t[:], initial=0.0, op0=ALU.mult, op1=ALU.add),
             reads=[lat, K], writes=[acst])
        for c in range(CPB):
            cs = slice(c * 128, (c + 1) * 128)
            pA, pB, pR0, pR1, pO, pD, pH, pC = ps
            for j in range(3):
                P.op("pe", lambda e, j=j, cs=cs: e.matmul(pA.t[:, j * 128:(j + 1) * 128], lhsT=cv[j].t[:, cs], rhs=identf, start=True, stop=True),
                     reads=[cv[j], K], writes=[pA])
            P.op("pe", lambda e, cs=cs: e.matmul(pA.t[:, 384:512], lhsT=BTb.t[:, cs], rhs=identb, start=True, stop=True), reads=[BTb, KB], writes=[pA])
            P.op("pe", lambda e, cs=cs: e.matmul(pB.t[:, 0:6], lhsT=dtt.t[:, cs], rhs=K.t[0:6, K_ID:K_ID + 6], start=True, stop=True), reads=[dtt, K], writes=[pB])
            P.op("pe", lambda e, cs=cs: e.matmul(pB.t[:, 8:14], lhsT=acst.t[:, cs], rhs=K.t[0:6, K_ID:K_ID + 6], start=True, stop=True), reads=[acst, K], writes=[pB])
            P.op("act", lambda e: e.activation(out=xstok.t[:], in_=pA.t[:, 0:384], func=AF.Copy), reads=[pA], writes=[xstok])
            P.op("dve", lambda e: e.tensor_copy(out=btok.t[:], in_=pA.t[:, 384:512]), reads=[pA], writes=[btok])
            P.op("dve", lambda e: e.tensor_copy(out=dttok.t[:], in_=pB.t[:, 0:6]), reads=[pB], writes=[dttok])
            P.op("dve", lambda e: e.tensor_copy(out=acstok.t[:], in_=pB.t[:, 8:14]), reads=[pB], writes=[acstok])
            for h in range(6):
                pr = pR0 if h < 4 else pR1
                hh = h % 4
                P.op("pe", lambda e, h=h, pr=pr, hh=hh, cs=cs: e.matmul(pr.t[:, hh * 128:(hh + 1) * 128], lhsT=K.t[0:6, K_E + h * 128:K_E + (h + 1) * 128],
                                                                       rhs=acst.t[:, cs], start=True, stop=True), reads=[K, acst], writes=[pr])
            P.op("pe", lambda e, cs=cs: e.matmul(pC.t[:, 0:128], lhsT=BTb.t[:, cs], rhs=CTb.t[:, cs], start=True, stop=True), reads=[BTb, CTb], writes=[pC])
            P.op("dve", lambda e: e.tensor_tensor(out=cbm.t[:], in0=pC.t[:, 0:128], in1=triu, op=ALU.mult), reads=[pC, K], writes=[cbm])
            P.op("dve", lambda e: e.tensor_tensor(out=diff.t[:, 0:512].rearrange("p (h t) -> p h t", t=128), in0=pR0.t[:, :].rearrange("p (h t) -> p h t", t=128),
                                                  in1=bc3(acstok.t[:, 0:4], 128), op=ALU.subtract), reads=[pR0, acstok], writes=[diff])
            P.op("dve", lambda e: e.tensor_tensor(out=diff.t[:, 512:768].rearrange("p (h t) -> p h t", t=128), in0=pR1.t[:, 0:256].rearrange("p (h t) -> p h t", t=128),
                                                  in1=bc3(acstok.t[:, 4:6], 128), op=ALU.subtract), reads=[pR1, acstok], writes=[diff])
            P.op("dve", lambda e: e.tensor_copy(out=eal.t[:, 0:4], in_=pR0.t[:, :].rearrange("p (h t) -> p h t", t=128)[:, :, 127]), reads=[pR0], writes=[eal])
            P.op("dve", lambda e: e.tensor_copy(out=eal.t[:, 4:6], in_=pR1.t[:, 0:256].rearrange("p (h t) -> p h t", t=128)[:, :, 127]), reads=[pR1], writes=[eal])
            P.op("dve", lambda e: e.tensor_tensor(out=dec.t[:], in0=eal.t[:], in1=acstok.t[:], op=ALU.subtract), reads=[eal, acstok], writes=[dec])
            P.op("act", lambda e: e.activation(out=diff.t[:], in_=diff.t[:], func=AF.Exp), reads=[diff], writes=[diff])
            P.op("act", lambda e: e.activation(out=dec.t[:], in_=dec.t[:], func=AF.Exp), reads=[dec], writes=[dec])
            P.op("act", lambda e: e.activation(out=eal.t[:], in_=eal.t[:], func=AF.Exp), reads=[eal], writes=[eal])
            P.op("act", lambda e: e.activation(out=ea.t[:], in_=acstok.t[:], func=AF.Exp), reads=[acstok], writes=[ea])
            P.op("dve", lambda e: e.scalar_tensor_tensor(out=Gt.t[:].rearrange("p (h t) -> p h t", t=128), in0=diff.t[:].rearrange("p (h t) -> p h t", t=128),
                                                         scalar=1.0, in1=cbm.t[:].unsqueeze(1).broadcast_to([128, 6, 128]), op0=ALU.min, op1=ALU.mult),
                 reads=[diff, cbm], writes=[Gt])
            P.op("dve", lambda e: e.tensor_tensor(out=xd.t[:].rearrange("p (h q) -> p h q", q=64), in0=xstok.t[:].rearrange("p (h q) -> p h q", q=64),
                                                  in1=bc3(dttok.t[:], 64), op=ALU.mult), reads=[xstok, dttok], writes=[xd])
            P.op("act", lambda e: e.activation(out=xdb.t[:], in_=xd.t[:], func=AF.Copy), reads=[xd], writes=[xdb])
            P.op("dve", lambda e: e.tensor_tensor(out=xddb.t[:].rearrange("p (h q) -> p h q", q=64), in0=xd.t[:].rearrange("p (h q) -> p h q", q=64),
                                                  in1=bc3(dec.t[:], 64), op=ALU.mult), reads=[xd, dec], writes=[xddb])
            P.op("pe", lambda e, cs=cs: e.matmul(pO.t[:, 0:384], lhsT=CTb.t[:, cs], rhs=Hb.t[:], start=True, stop=True), reads=[CTb, Hb], writes=[pO])
            for h in range(6):
                P.op("pe", lambda e, h=h: e.matmul(pD.t[:, h * 64:(h + 1) * 64], lhsT=Gt.t[:, h * 128:(h + 1) * 128], rhs=xdb.t[:, h * 64:(h + 1) * 64],
                                                   start=True, stop=True), reads=[Gt, xdb], writes=[pD])
            P.op("pe", lambda e: e.matmul(pH.t[:, 0:384], lhsT=btok.t[:], rhs=xddb.t[:], start=True, stop=True), reads=[btok, xddb], writes=[pH])
            P.op("dve", lambda e: e.tensor_tensor(out=t1.t[:].rearrange("p (h q) -> p h q", q=64), in0=pO.t[:, 0:384].rearrange("p (h q) -> p h q", q=64),
                                                  in1=bc3(ea.t[:], 64), op=ALU.mult), reads=[pO, ea], writes=[t1])
            P.op("dve", lambda e: e.tensor_tensor(out=t1.t[:], in0=t1.t[:], in1=pD.t[:, 0:384], op=ALU.add), reads=[t1, pD], writes=[t1])
            P.op("dve", lambda e: e.tensor_tensor(out=t2.t[:], in0=xstok.t[:], in1=K.t[:, K_D:K_D + 384], op=ALU.mult), reads=[xstok, K], writes=[t2])
            P.op("dve", lambda e: e.tensor_tensor(out=t1.t[:], in0=t1.t[:], in1=t2.t[:], op=ALU.add), reads=[t1, t2], writes=[t1])
            P.op("dve", lambda e: e.tensor_tensor(out=H.t[:].rearrange("p (h q) -> p h q", q=64), in0=H.t[:].rearrange("p (h q) -> p h q", q=64),
                                                  in1=bc3(eal.t[:], 64), op=ALU.mult), reads=[H, eal], writes=[H])
            P.op("dve", lambda e: e.tensor_tensor(out=H.t[:], in0=H.t[:], in1=pH.t[:, 0:384], op=ALU.add), reads=[H, pH], writes=[H])
            P.op("act", lambda e: e.activation(out=Hb.t[:], in_=H.t[:], func=AF.Copy), reads=[H], writes=[Hb])
            for j in range(3):
                P.op("pe", lambda e, j=j: e.matmul(pB.t[:, 128 + j * 128:128 + (j + 1) * 128] if False else pC.t[:, 128 + j * 128:256 + j * 128],
                                                   lhsT=t1.t[:, j * 128:(j + 1) * 128], rhs=identf, start=True, stop=True), reads=[t1, K], writes=[pC])
            for j in range(3):
                P.op("act", lambda e, j=j, cs=cs: e.activation(out=yTb[j].t[:, cs], in_=pC.t[:, 128 + j * 128:256 + j * 128], func=AF.Copy),
                     reads=[pC], writes=[yTb[j]])
        for j in range(3):
            P.dma("sp", yT.t[j, :, b0:b0 + BL1], yTb[j].t[:], reads=[yTb[j]], writes=[yT.sub((j, blk))], sem=yTb[j])


POOLW = (2, 4, 8, 16)


def consts_B1(inp, c):
    kf = np.zeros((128, K_N), np.float32)
    kf[:, K_ID:K_ID + 128] = np.eye(128)
    s = np.arange(128)[:, None]
    t = np.arange(128)[None, :]
    kf[:, K_TRI:K_TRI + 128] = (s <= t)
    rm = np.ones(BL1, np.float32)
    rm[0::128] = 0
    kf[:, K_RM:K_RM + BL1] = rm[None, :]
    cw = inp["ssm_conv_w"][0]
    cb = inp["ssm_conv_b"][0]
    chans = [np.arange(384 * c + 128 * j, 384 * c + 128 * j + 128) for j in range(3)] + \
            [3072 + np.arange(128 * c, 128 * c + 128), 4096 + np.arange(128 * c, 128 * c + 128)]
    for i, ch in enumerate(chans):
        kf[:, K_CW + 4 * i:K_CW + 4 * i + 4] = cw[:, ch].T
        kf[:, K_CB + i] = cb[ch]
    g = c // 2
    w = POOLW[g]
    kf[:, K_SELW + g] = 1.0 / w
    tt = np.arange(16)
    kf[:, K_CORR:K_CORR + 16] = (w / np.minimum(tt + 1, w))[None, :]
    kf[:, K_D:K_D + 384] = np.repeat(inp["ssm_d"][0][6 * c:6 * c + 6], 64)[None, :]
    kf[0:6, K_DTA] = inp["ssm_dt_bias"][0][6 * c:6 * c + 6]
    kf[0:6, K_DTA + 1] = inp["ssm_a_log"][0][6 * c:6 * c + 6]
    for h in range(6):
        kf[h, K_E + 128 * h:K_E + 128 * (h + 1)] = 1.0
    return kf, np.eye(128, dtype=np.float32).astype(NPBF)


def assemble(outs):
    rows = outs[0].shape[0]
    full = np.zeros((rows, LP), outs[0].dtype)
    full[:, PADL:PADL + TOK] = outs[0]
    for c in range(1, NCORE):
        full[:, PADL + 1024 * c + 16:PADL + 1024 * c + TOK] = outs[c][:, 16:]
    return full


def host_B1(inp, pF1, LPv=LP):
    full = assemble(pF1)[:, :LPv]
    maps = []
    for c in range(NCORE):
        kf, kb = consts_B1(inp, c)
        xin = np.stack([full[4096 + 384 * c + 128 * j:4096 + 384 * c + 128 * j + 128] for j in range(3)] +
                       [full[7168 + 128 * c:7168 + 128 * c + 128], full[8192 + 128 * c:8192 + 128 * c + 128]])
        maps.append({"pu": np.ascontiguousarray(full[128 * c:128 * c + 128]), "xin": np.ascontiguousarray(xin),
                     "dtr": np.ascontiguousarray(full[9216 + 6 * c:9216 + 6 * c + 6]), "kf": kf, "kb": kb})
    return maps


def _run(build_fn, maps):
    nc = bass.Bass("TRN2", target_bir_lowering=False)
    with contextlib.ExitStack() as stack:
        P = Prog(nc, stack)
        build_fn(nc, P)
        P.wait_all_dma("sp")
        P.finalize()
    res = run_bass_kernel_spmd(nc, maps, core_ids=list(range(NCORE)))
    return res.results


def kernel(**inputs):
    inp = {k: np.asarray(v) for k, v in inputs.items()}
    rA = _run(build_A0, host_A0(inp))
    oF = [r["oF"] for r in rA]
    oB = [np.asarray(r["oB"]).astype(NPBF) for r in rA]
    rB = _run(lambda nc, P: build_B0(nc, P, LP), host_B0(inp, oF, oB, LP))
    oa = [r["oa"] for r in rB]
    ob = [np.asarray(r["ob"]).astype(NPBF) for r in rB]
    del rA, rB
    rC = _run(lambda nc, P: build_C(nc, P, 0), host_C0(inp, oF, oa, ob))
    h1T = [r["h1T"] for r in rC]
    pF1 = [r["pF1"] for r in rC]
    rB1 = _run(lambda nc, P: build_B1(nc, P, LP), host_B1(inp, pF1, LP))
    pooled = [r["pooled"] for r in rB1]
    ys = [r["yT"] for r in rB1]
    rC1 = _run(lambda nc, P: build_C(nc, P, 1), host_C1(inp, pF1, h1T, pooled, ys))
    out = np.concatenate([np.asarray(r["h2T"])[:, 16:TOK].T for r in rC1], axis=0)
    return np.ascontiguousarray(out.reshape(1, SEQ, D).astype(np.float32))
```
